# Optimizing a Trainium2 kernel written in Bass

```python
import math
import jax, jax.numpy as jnp
from jax import lax
import numpy as np

D_MODEL = 1024
BATCH = 16
SEQ = 2048
DEPTH = 4
DEC_BATCH = 4
DEC_SEQ = 4096
PAST_LEN = 128

N_HEADS = 4
HEAD_DIM = 64
V_DIM = 2 * HEAD_DIM
QK_W = N_HEADS * 2 * HEAD_DIM
ATTN_W = N_HEADS * V_DIM
N_FGROUPS = 4
FGROUP_DIM = 128
FOURIER_W = N_FGROUPS * FGROUP_DIM
IN_W = 2 * QK_W + ATTN_W + FOURIER_W
D_FF = 2816
ROPE_THETA = 10000.0
Q_BLOCK = 128
EPS = 1e-6
LAMBDA_STD = 0.1

kernel_name = "hybrid_diffattn_fnet_macaron_encoder"


def rmsnorm(x, g):
    xf = x.astype(jnp.float32)
    y = xf * lax.rsqrt(jnp.mean(xf * xf, axis=-1, keepdims=True) + EPS)
    return (y * g.astype(jnp.float32)).astype(x.dtype)


def swiglu(h, w_up, w_down):
    gate, up = jnp.split(h @ w_up, 2, axis=-1)
    return (jax.nn.silu(gate) * up) @ w_down


def rope_tables(seq, dtype):
    inv = 1.0 / (ROPE_THETA ** (jnp.arange(0, HEAD_DIM, 2, dtype=jnp.float32) / HEAD_DIM))
    ang = jnp.arange(seq, dtype=jnp.float32)[:, None] * inv[None, :]
    ang = jnp.concatenate([ang, ang], axis=-1)
    return jnp.cos(ang).astype(dtype), jnp.sin(ang).astype(dtype)


def apply_rope(t, cos, sin):
    t1, t2 = jnp.split(t, 2, axis=-1)
    rot = jnp.concatenate([-t2, t1], axis=-1)
    c = cos[None, :, None, None, :]
    s = sin[None, :, None, None, :]
    return t * c + rot * s


def diff_attention(q, k, v, lam):
    B, S = q.shape[0], q.shape[1]
    nb = S // Q_BLOCK
    qb = q.reshape(B, nb, Q_BLOCK, N_HEADS, 2, HEAD_DIM).transpose(1, 0, 2, 3, 4, 5)
    scale = HEAD_DIM ** -0.5

    def block(q_blk):
        s = jnp.einsum('bqhcd,bkhcd->bhcqk', q_blk, k).astype(jnp.float32) * scale
        p = jax.nn.softmax(s, axis=-1)
        a = p[:, :, 0] - lam * p[:, :, 1]
        return jnp.einsum('bhqk,bkhe->bqhe', a.astype(v.dtype), v)

    o = lax.map(block, qb)
    return o.transpose(1, 0, 2, 3, 4).reshape(B, S, N_HEADS, V_DIM)


def fourier_mix(u):
    B, S = u.shape[0], u.shape[1]
    ug = u.reshape(B, S, N_FGROUPS, FGROUP_DIM).astype(jnp.float32)
    f = jnp.fft.fftn(ug, axes=(1, 3), norm="ortho").real
    return f.reshape(B, S, FOURIER_W).astype(u.dtype)


def trunk(x, g_ff1, w_ff1_up, w_ff1_down, g_mix, w_in, lam_q1, lam_k1, lam_q2, lam_k2,
          g_sub, w_pa, w_pf, w_gate, w_o, g_ff2, w_ff2_up, w_ff2_down, g_final):
    B, S, _ = x.shape
    cos, sin = rope_tables(S, x.dtype)
    for l in range(DEPTH):
        x = x + 0.5 * swiglu(rmsnorm(x, g_ff1[l]), w_ff1_up[l], w_ff1_down[l])
        h = rmsnorm(x, g_mix[l])
        proj = h @ w_in[l]
        q, k, v, u = jnp.split(proj, [QK_W, 2 * QK_W, 2 * QK_W + ATTN_W], axis=-1)
        q = apply_rope(q.reshape(B, S, N_HEADS, 2, HEAD_DIM), cos, sin)
        k = apply_rope(k.reshape(B, S, N_HEADS, 2, HEAD_DIM), cos, sin)
        v = v.reshape(B, S, N_HEADS, V_DIM)
        lam_init = 0.8 - 0.6 * math.exp(-0.3 * l)
        lam = (jnp.exp(jnp.sum(lam_q1[l].astype(jnp.float32) * lam_k1[l].astype(jnp.float32)))
               - jnp.exp(jnp.sum(lam_q2[l].astype(jnp.float32) * lam_k2[l].astype(jnp.float32)))
               + lam_init)
        o = diff_attention(q, k, v, lam)
        o = rmsnorm(o, g_sub[l]) * (1.0 - lam_init)
        br_a = o.reshape(B, S, ATTN_W) @ w_pa[l]
        br_f = fourier_mix(u) @ w_pf[l]
        g_a, g_f = jnp.split(jax.nn.sigmoid(h @ w_gate[l]), 2, axis=-1)
        x = x + (g_a * br_a + g_f * br_f) @ w_o[l]
        x = x + 0.5 * swiglu(rmsnorm(x, g_ff2[l]), w_ff2_up[l], w_ff2_down[l])
    return rmsnorm(x, g_final)


def setup_inputs(seed: int = 0) -> dict:
    key = jax.random.key(seed)
    ks = jax.random.split(key, 24)
    f32 = jnp.float32

    def w(k, shape, fan_in):
        return jax.random.normal(k, shape, f32) * (fan_in ** -0.5)

    def gain(k, shape):
        return 1.0 + 0.02 * jax.random.normal(k, shape, f32)

    return {
        "x_prompt": jax.random.normal(ks[0], (BATCH, SEQ, D_MODEL), f32),
        "x_sample": jax.random.normal(ks[1], (DEC_BATCH, DEC_SEQ, D_MODEL), f32),
        "g_ff1": gain(ks[2], (DEPTH, D_MODEL)),
        "w_ff1_up": w(ks[3], (DEPTH, D_MODEL, 2 * D_FF), D_MODEL),
        "w_ff1_down": w(ks[4], (DEPTH, D_FF, D_MODEL), D_FF),
        "g_mix": gain(ks[5], (DEPTH, D_MODEL)),
        "w_in": w(ks[6], (DEPTH, D_MODEL, IN_W), D_MODEL),
        "lam_q1": LAMBDA_STD * jax.random.normal(ks[7], (DEPTH, HEAD_DIM), f32),
        "lam_k1": LAMBDA_STD * jax.random.normal(ks[8], (DEPTH, HEAD_DIM), f32),
        "lam_q2": LAMBDA_STD * jax.random.normal(ks[9], (DEPTH, HEAD_DIM), f32),
        "lam_k2": LAMBDA_STD * jax.random.normal(ks[10], (DEPTH, HEAD_DIM), f32),
        "g_sub": gain(ks[11], (DEPTH, V_DIM)),
        "w_pa": w(ks[12], (DEPTH, ATTN_W, D_MODEL), ATTN_W),
        "w_pf": w(ks[13], (DEPTH, FOURIER_W, D_MODEL), FOURIER_W),
        "w_gate": w(ks[14], (DEPTH, D_MODEL, 2 * D_MODEL), D_MODEL),
        "w_o": w(ks[15], (DEPTH, D_MODEL, D_MODEL), D_MODEL),
        "g_ff2": gain(ks[16], (DEPTH, D_MODEL)),
        "w_ff2_up": w(ks[17], (DEPTH, D_MODEL, 2 * D_FF), D_MODEL),
        "w_ff2_down": w(ks[18], (DEPTH, D_FF, D_MODEL), D_FF),
        "g_final": gain(ks[19], (D_MODEL,)),
    }


def reference(x_prompt, x_sample, g_ff1, w_ff1_up, w_ff1_down, g_mix, w_in, lam_q1, lam_k1,
              lam_q2, lam_k2, g_sub, w_pa, w_pf, w_gate, w_o, g_ff2, w_ff2_up, w_ff2_down, g_final):
    y_prompt = trunk(x_prompt, g_ff1, w_ff1_up, w_ff1_down, g_mix, w_in, lam_q1, lam_k1, lam_q2,
                     lam_k2, g_sub, w_pa, w_pf, w_gate, w_o, g_ff2, w_ff2_up, w_ff2_down, g_final)
    y_sample = trunk(x_sample, g_ff1, w_ff1_up, w_ff1_down, g_mix, w_in, lam_q1, lam_k1, lam_q2,
                     lam_k2, g_sub, w_pa, w_pf, w_gate, w_o, g_ff2, w_ff2_up, w_ff2_down, g_final)
    return (y_prompt, y_sample)
```

```python
import math
from contextlib import ExitStack

import numpy as np
import ml_dtypes

import concourse.bass as bass
import concourse.mybir as mybir
from concourse.bass_utils import run_bass_kernel_spmd

F32 = mybir.dt.float32
BF16 = mybir.dt.bfloat16
AF = mybir.ActivationFunctionType
ALU = mybir.AluOpType
AX = mybir.AxisListType

P = 128
D = 1024
KC = 8
DFF = 2816
FC = 22
T = 512
NH = 4
EPS = 1e-6
SEM_LIMIT = 30000
NEG = -30000.0

FULL_CFG = dict(depth=4, ch=2048)


class Reg:
    __slots__ = ("w", "r", "lsem", "ssem", "name")

    def __init__(self, name=""):
        self.w = None
        self.r = []
        self.lsem = None
        self.ssem = None
        self.name = name


class Sched:
    ENGS = ("pe", "act", "dve", "pool", "sp")

    def __init__(self, nc, es):
        self.nc = nc
        self.es = es
        self.q = {e: [] for e in self.ENGS}
        self.cur = {}
        self.waited = {e: {} for e in self.ENGS}
        self.nsem = 0
        self.epoch = 0
        self.active = []
        self.free_holders = []
        self.semobjs = []

    def new_sem(self):
        self.nsem += 1
        s = self.es.enter_context(self.nc.semaphore(f"s{self.nsem}"))
        self.semobjs.append(s)
        return (self.nsem, s)

    def _filter(self, eng, waits):
        wl = []
        w = self.waited[eng]
        for tk in waits:
            if tk is None:
                continue
            sem, val, ep = tk
            if ep < self.epoch:
                continue
            k = sem[0]
            if w.get(k, 0) >= val:
                continue
            w[k] = val
            wl.append((sem[1], val))
        return wl

    def op(self, eng, fns, waits=(), tok=None, signal=True):
        if callable(fns):
            fns = [fns]
        wl = self._filter(eng, waits)
        kind = "eng"
        if tok is not None:
            kind = "dma"
        elif signal:
            c = self.cur.get(eng)
            if c is None or c[1] >= SEM_LIMIT:
                c = [self.new_sem(), 0]
                self.cur[eng] = c
            c[1] += 1
            tok = (c[0], c[1], self.epoch)
        self.q[eng].append((wl, fns, tok, kind))
        return tok

    def run(self, eng, fns, reads=(), writes=(), extra=()):
        waits = list(extra)
        for g in reads:
            waits.append(g.w)
        for g in writes:
            waits.append(g.w)
            waits.extend(g.r)
        tok = self.op(eng, fns, waits)
        for g in reads:
            g.r.append(tok)
        for g in writes:
            g.w = tok
            g.r = []
        return tok

    def _holder(self, old, longlived):
        if old is not None and (old["long"] or old["epoch"] == self.epoch):
            return old
        if longlived:
            return dict(sem=self.new_sem(), count=0, epoch=self.epoch, long=True)
        if self.free_holders:
            h = self.free_holders.pop()
            h["epoch"] = self.epoch
        else:
            h = dict(sem=self.new_sem(), count=0, epoch=self.epoch, long=False)
        self.active.append(h)
        return h

    def dma(self, qeng, outs_ins, sb_reg, load, reads=(), writes=(), longlived=False):
        if load:
            holder = sb_reg.lsem = self._holder(sb_reg.lsem, longlived)
        else:
            holder = sb_reg.ssem = self._holder(sb_reg.ssem, longlived)
        waits = []
        rds = list(reads) + ([] if load else [sb_reg])
        wrs = list(writes) + ([sb_reg] if load else [])
        for g in rds:
            waits.append(g.w)
        for g in wrs:
            waits.append(g.w)
            waits.extend(g.r)
        ep = (1 << 60) if holder["long"] else self.epoch
        waits.append((holder["sem"], holder["count"] * 16, ep))
        fns = []
        for (o, i) in outs_ins:
            fns.append(lambda e, o=o, i=i: e.dma_start(out=o, in_=i))
        holder["count"] += len(fns)
        tok = (holder["sem"], holder["count"] * 16, ep)
        self.op(qeng, fns, waits, tok=tok)
        for g in rds:
            g.r.append(tok)
        for g in wrs:
            g.w = tok
            g.r = []
        return tok

    def barrier(self):
        toks = []
        for e, c in self.cur.items():
            toks.append((c[0], c[1], self.epoch))
        for h in self.active:
            if h["count"] > 0:
                toks.append((h["sem"], h["count"] * 16, self.epoch))
        for e in self.ENGS:
            self.op(e, [], toks, signal=False)

    def end_phase(self):
        self.free_holders.extend(self.active)
        self.active = []
        self.epoch += 1

    def emit(self):
        nc = self.nc
        with nc.Block() as block:

            def mk(name):
                def body(e):
                    for wl, fns, tok, kind in self.q[name]:
                        for sem, val in wl:
                            e.wait_ge(sem, val)
                        n = len(fns)
                        for i, f in enumerate(fns):
                            ins = f(e)
                            if kind == "dma":
                                ins.then_inc(tok[0][1], 16)
                            elif tok is not None and i == n - 1:
                                ins.then_inc(tok[0][1], 1)
                return body

            block.tensor(mk("pe"))
            block.scalar(mk("act"))
            block.vector(mk("dve"))
            block.gpsimd(mk("pool"))
            block.sync(mk("sp"))
        self.q = {e: [] for e in self.ENGS}


def _pieces(W, col_groups):
    K = W.shape[0]
    kc = K // P
    out = []
    for cols in col_groups:
        sub = W[:, cols]
        sub = sub.reshape(kc, P, len(cols)).transpose(1, 0, 2).reshape(P, kc * len(cols))
        out.append(sub)
    return np.ascontiguousarray(np.stack(out, 0))


def _rot_perm():
    idx = np.arange(512)
    g = idx // 64
    d = idx % 64
    return g * 64 + (d + 32) % 64


def _layout_weights(inp, L):
    up_groups = []
    for i in range(11):
        cols = np.concatenate([np.arange(256 * i, 256 * i + 256), DFF + np.arange(256 * i, 256 * i + 256)])
        up_groups.append(cols)
    down_groups = [np.arange(128 * j, 128 * j + 128) for j in range(8)]
    w_up, w_down, w_inp, w_gate, w_papf, w_o = [], [], [], [], [], []
    rp = _rot_perm()
    for l in range(L):
        for nm in ("ff1", "ff2"):
            w_up.append(_pieces(inp[f"w_{nm}_up"][l], up_groups))
            w_down.append(_pieces(inp[f"w_{nm}_down"][l], down_groups))
        wi = inp["w_in"][l]
        groups = [np.arange(0, 512), rp, 512 + np.arange(512), 512 + rp, 1536 + np.arange(512), 1024 + np.arange(512)]
        w_inp.append(_pieces(wi, groups))
        w_gate.append(_pieces(inp["w_gate"][l], [np.arange(512 * i, 512 * i + 512) for i in range(4)]))
        w_papf.append(np.concatenate([_pieces(inp["w_pa"][l], [np.arange(1024)]), _pieces(inp["w_pf"][l], [np.arange(1024)])], 0))
        w_o.append(_pieces(inp["w_o"][l], [np.arange(512 * i, 512 * i + 512) for i in range(2)]))
    return dict(
        w_up=np.stack(w_up, 0), w_down=np.stack(w_down, 0), w_inp=np.stack(w_inp, 0),
        w_gate=np.stack(w_gate, 0), w_papf=np.stack(w_papf, 0), w_o=np.stack(w_o, 0),
    )


class Builder:
    def __init__(self, cfg):
        self.cfg = cfg
        self.L = cfg["depth"]
        self.CH = cfg["ch"]
        self.NTC = self.CH // T
        self.NT = 3 * self.NTC
        self.NTOK = 3 * self.CH
        self.mode = cfg.get("mode", "full")

    def sb(self, name, shape, dt):
        return self.es.enter_context(self.nc.sbuf_tensor(name, list(shape), dt))

    def psb(self, name, shape, dt):
        self.uid += 1
        return self.pes.enter_context(self.nc.sbuf_tensor(f"{name}_{self.uid}", list(shape), dt))

    def phase(self, fn):
        with ExitStack() as pes:
            self.pes = pes
            fn()
            self.S.barrier()
            self.S.emit()
            self.S.end_phase()

    def ps_next(self):
        b = self.ps_i % 8
        self.ps_i += 1
        return self.ps[:, b, :], self.psr[b]

    def mm(self, out_ap, bank_reg, pairs, reads):
        n = len(pairs)
        fns = [
            (lambda e, l=l, r=r, i=i: e.matmul(out_ap, l, r, start=(i == 0), stop=(i == n - 1)))
            for i, (l, r) in enumerate(pairs)
        ]
        return self.S.run("pe", fns, reads=reads, writes=[bank_reg])

    def build(self):
        cfg = self.cfg
        L, CH, NT, NTOK = self.L, self.CH, self.NT, self.NTOK
        nc = bass.Bass("TRN2", target_bir_lowering=False)
        self.nc = nc

        def din(name, shape, dt=F32):
            return nc.dram_tensor(name, list(shape), dt, kind="ExternalInput").ap()

        def dscr(name, shape, dt):
            kind = "ExternalOutput" if (cfg.get("debug") and name in ("qkT", "vtok", "uT", "oT", "fT", "xT")) else "Internal"
            return nc.dram_tensor(name, list(shape), dt, kind=kind).ap()

        self.x_in = din("x_in", [NTOK, D])
        self.wf = dict(
            w_up=din("w_up", [2 * L, 11, P, 4096]), w_down=din("w_down", [2 * L, 8, P, 2816]),
            w_inp=din("w_inp", [L, 6, P, 4096]), w_gate=din("w_gate", [L, 4, P, 4096]),
            w_papf=din("w_papf", [L, 2, P, 4096]), w_o=din("w_o", [L, 2, P, 4096]),
        )
        self.gvec = din("gvec", [P, (3 * L + 1) * 8])
        self.gsub = din("gsub", [P, L])
        self.lamv = din("lamv", [4, L, 64])
        self.rope = din("rope", [2, P, NTOK])
        self.xbias = din("xbias", [P, 1])
        self.ident_in = din("ident_in", [P, P])
        self.cs5 = din("cs5", [5, P, 256])
        self.mtab = din("mtab", [3, 2, CH, CH], BF16)
        self.y_out = nc.dram_tensor("y_out", [NTOK, D], F32, kind="ExternalOutput").ap()

        self.wb = {k: dscr("b_" + k, v.shape, BF16) for k, v in self.wf.items()}
        self.xT = dscr("xT", [KC, P, NTOK], F32)
        self.qkT = dscr("qkT", [8, P, NTOK], BF16)
        self.vtok = dscr("vtok", [NTOK, 512], BF16)
        self.uT = dscr("uT", [4, P, NTOK], BF16)
        self.oT = dscr("oT", [4, P, NTOK], BF16)
        self.fT = dscr("fT", [4, P, NTOK], BF16)

        with ExitStack() as es:
            self.es = es
            S = Sched(nc, es)
            self.S = S
            self.ps = es.enter_context(nc.psum_tensor("ps", [P, 8, 512], F32))
            self.psr = [Reg(f"ps{b}") for b in range(8)]
            self.ps_i = 0
            self.uid = 0
            self.phase(lambda: (self.setup_consts(), self.convert_weights()))
            for p in range(L + 1):
                self.phase(lambda p=p: self.row_pass(p))
                if p < L and self.mode == "full":
                    self.phase(lambda p=p: self.attention(p))
                    self.phase(lambda p=p: self.fourier(p))
        return nc

    def setup_consts(self):
        S, nc, L = self.S, self.nc, self.L
        L = self.L
        ng = (3 * L + 1) * 8
        self.g32 = self.sb("g32", [P, ng], F32)
        self.g32r = Reg("g32")
        self.ones = self.sb("ones", [P, P], BF16)
        self.onesr = Reg("ones")
        self.ident = self.sb("ident", [P, P], F32)
        self.identr = Reg("ident")
        self.zero1 = self.sb("zero1", [P, 1], F32)
        self.zero1r = Reg("zero1")
        S.dma("sp", [(self.g32[:], self.gvec[:, :])], self.g32r, True)
        self.epst = self.sb("epst", [P, 1], F32)
        self.epsr = Reg("eps")
        S.run("pool", lambda e: e.memset(self.epst[:], EPS), writes=[self.epsr])
        S.run("pool", lambda e: e.memset(self.ones[:], 1.0), writes=[self.onesr])
        S.run("pool", lambda e: e.memset(self.zero1[:], 0.0), writes=[self.zero1r])
        S.dma("sp", [(self.ident[:], self.ident_in[:, :])], self.identr, True)
        self.neglam = self.sb("neglam", [P, L], F32)
        self.gs = self.sb("gs", [P, L], F32)
        self.scalr = Reg("scal")
        ssum = self.sb("lss", [P, 2 * L], F32)
        esum = self.sb("les", [P, 2 * L], F32)
        self.xb = self.sb("xb", [P, 1], F32)
        self.cs5t = self.sb("cs5t", [P, 5, 256], BF16)
        lv = self.psb("lv", [P, 4 * L * 64], F32)
        lvr = Reg("lv")
        S.dma("sp", [(lv[:], self.lamv.rearrange("a l d -> (a l d)").partition_broadcast(P))], lvr, True)
        S.dma("sp", [(self.gs[:], self.gsub[:, :])], self.scalr, True)
        pr = self.psb("lpr", [P, 64], F32)
        prr, ssr, esr = Reg("pr"), Reg("ss"), Reg("es")
        for l in range(L):
            lam_init = 0.8 - 0.6 * math.exp(-0.3 * l)
            for a in range(2):
                o1 = ((2 * a) * L + l) * 64
                o2 = ((2 * a + 1) * L + l) * 64
                S.run("dve", lambda e, o1=o1, o2=o2: e.tensor_tensor(out=pr[:], in0=lv[:, o1:o1 + 64], in1=lv[:, o2:o2 + 64], op=ALU.mult),
                      reads=[lvr], writes=[prr])
                S.run("dve", lambda e, a=a, l=l: e.reduce_sum(out=ssum[:, 2 * l + a:2 * l + a + 1], in_=pr[:], axis=AX.X),
                      reads=[prr], writes=[ssr])
            S.run("act", lambda e, l=l: e.activation(out=esum[:, 2 * l:2 * l + 2], in_=ssum[:, 2 * l:2 * l + 2], func=AF.Exp),
                  reads=[ssr], writes=[esr])
            S.run("dve", lambda e, l=l: e.tensor_tensor(out=self.neglam[:, l:l + 1], in0=esum[:, 2 * l + 1:2 * l + 2],
                                                        in1=esum[:, 2 * l:2 * l + 1], op=ALU.subtract),
                  reads=[esr], writes=[self.scalr])
            S.run("dve", lambda e, l=l, li=lam_init: e.tensor_scalar_add(self.neglam[:, l:l + 1], self.neglam[:, l:l + 1], -li),
                  writes=[self.scalr])
            S.run("dve", lambda e, l=l, li=lam_init: e.tensor_scalar_mul(self.gs[:, l:l + 1], self.gs[:, l:l + 1], 1.0 - li),
                  writes=[self.scalr])
        self.xbr = Reg("xb")
        S.dma("sp", [(self.xb[:], self.xbias[:, :])], self.xbr, True)
        self.cs5r = Reg("cs5")
        S.dma("pool", [(self.cs5t[:], self.cs5.rearrange("a p n -> p a n"))], self.cs5r, True)

    def convert_weights(self):
        self.conv = {}
        self.issue_conv(0)

    def conv_key(self, name, a):
        if name in ("w_up", "w_down"):
            return (a // 2, "ff1" if a % 2 == 0 else "ff2")
        return (a, "inp" if name == "w_inp" else "mix")

    def issue_conv(self, l):
        S = self.S
        groups = {"ff1": [], "inp": [], "mix": [], "ff2": []}
        for k in ("w_up", "w_down"):
            for f, gname in ((0, "ff1"), (1, "ff2")):
                for i in range(self.wf[k].shape[1]):
                    groups[gname].append((self.wb[k][2 * l + f, i], self.wf[k][2 * l + f, i]))
        for i in range(6):
            groups["inp"].append((self.wb["w_inp"][l, i], self.wf["w_inp"][l, i]))
        for k in ("w_gate", "w_papf", "w_o"):
            for i in range(self.wf[k].shape[1]):
                groups["mix"].append((self.wb[k][l, i], self.wf[k][l, i]))
        for gname in ("ff1", "inp", "mix", "ff2"):
            r = Reg(f"conv{l}{gname}")
            S.dma("pool", groups[gname], r, True, longlived=True)
            self.conv[(l, gname)] = r

    def ring(self, name, n, shape, dt):
        tiles = [self.psb(f"{name}{i}", shape, dt) for i in range(n)]
        regs = [Reg(f"{name}{i}") for i in range(n)]
        state = {"i": 0}

        def nxt():
            i = state["i"] % n
            state["i"] += 1
            return tiles[i], regs[i]

        return nxt

    class WRing:
        def __init__(self, B, nslot, plan):
            self.B = B
            self.n = nslot
            self.tiles = [B.psb(f"w{i}", [P, 4096], BF16) for i in range(nslot)]
            self.regs = [Reg(f"w{i}") for i in range(nslot)]
            self.plan = plan
            self.issued = 0
            self.used = 0

        def _issue(self, i):
            B = self.B
            name, a, b = self.plan[i]
            src = B.wb[name][a, b]
            n = src.shape[-1]
            slot = i % self.n
            B.S.dma("sp", [(self.tiles[slot][:, 0:n], src)], self.regs[slot], True, reads=[B.conv[B.conv_key(name, a)]])

        def next(self, tag):
            while self.issued < min(len(self.plan), self.used + self.n - 1):
                self._issue(self.issued)
                self.issued += 1
            i = self.used
            assert self.plan[i] == tag, (self.plan[i], tag)
            self.used += 1
            return self.tiles[i % self.n], self.regs[i % self.n]

    def norm(self, Xt, Xr, gcol, out, out_regs):
        S = self.S
        bank, br = self.ps_next()
        for k in range(KC):
            sq, sqr = self.SQ()
            S.run("act", lambda e, sq=sq, k=k: e.activation(out=sq[:], in_=Xt[:, k, :], func=AF.Square),
                  reads=[Xr[k]], writes=[sqr])
            S.run("pe", lambda e, sq=sq, k=k: e.matmul(bank, self.ones[:], sq[:], start=(k == 0), stop=(k == KC - 1)),
                  reads=[sqr, self.onesr], writes=([br] if k in (0, KC - 1) else []))
        rs0, rs0r = self.RS()
        rs, rsr = self.RS()
        S.run("act", lambda e: e.activation(out=rs0[:], in_=bank, func=AF.Sqrt, bias=self.epst[:], scale=1.0 / D),
              reads=[br, self.epsr], writes=[rs0r])
        S.run("dve", lambda e: e.reciprocal(out=rs[:], in_=rs0[:]), reads=[rs0r], writes=[rsr])
        for k in range(KC):
            S.run("dve", lambda e, k=k: e.scalar_tensor_tensor(
                out=out[:, k, :], in0=Xt[:, k, :], scalar=self.g32[:, gcol + k:gcol + k + 1], in1=rs[:],
                op0=ALU.mult, op1=ALU.mult), reads=[Xr[k], rsr, self.g32r], writes=[out_regs[k]])

    def ffn(self, Xt, Xr, gcol, widx, H, Hr, prenormed=False, hook=None):
        S, W = self.S, self.W
        A, Ar = self.A, self.Ar
        if not prenormed:
            self.norm(Xt, Xr, gcol, H, Hr)
        for i in range(11):
            Wt, Wr = W.next(("w_up", widx, i))
            for jj in range(2):
                j = 2 * i + jj
                bg, bgr = self.ps_next()
                self.mm(bg, bgr, [(Wt[:, k * 512 + jj * 128:k * 512 + jj * 128 + 128], H[:, k, :]) for k in range(KC)],
                        reads=[Wr] + Hr)
                bu, bur = self.ps_next()
                self.mm(bu, bur, [(Wt[:, k * 512 + 256 + jj * 128:k * 512 + 256 + jj * 128 + 128], H[:, k, :]) for k in range(KC)],
                        reads=[Wr] + Hr)
                sg, sgr = self.SG()
                S.run("act", lambda e, sg=sg, bg=bg: e.activation(out=sg[:], in_=bg, func=AF.Silu), reads=[bgr], writes=[sgr])
                S.run("dve", lambda e, sg=sg, bu=bu, j=j: e.tensor_tensor(out=A[:, j, :], in0=bu, in1=sg[:], op=ALU.mult),
                      reads=[bur, sgr], writes=[Ar[j]])
        if hook is not None:
            hook()
        for j in range(KC):
            Wd, Wdr = W.next(("w_down", widx, j))
            b, br = self.ps_next()
            self.mm(b, br, [(Wd[:, k * 128:(k + 1) * 128], A[:, k, :]) for k in range(FC)], reads=[Wdr] + Ar)
            S.run("dve", lambda e, b=b, j=j: e.scalar_tensor_tensor(
                out=Xt[:, j, :], in0=b, scalar=0.5, in1=Xt[:, j, :], op0=ALU.mult, op1=ALU.add),
                reads=[br], writes=[Xr[j]])

    def row_pass(self, p):
        S, L, NT = self.S, self.L, self.NT
        first, last = p == 0, p == L
        post, pre = p >= 1, p < L
        full = self.mode == "full"
        lp, ln = p - 1, p

        def gcol(l, which):
            return (3 * l + which) * 8

        gfin = 3 * L * 8
        X = [self.psb(f"X{i}", [P, KC, T], F32) for i in range(2)]
        Xr = [[Reg(f"X{i}_{k}") for k in range(KC)] for i in range(2)]
        Hs = [self.psb(f"H{i}", [P, KC, T], BF16) for i in range(2)]
        Hrs = [[Reg(f"H{i}_{k}") for k in range(KC)] for i in range(2)]
        self.A = self.psb("A", [P, FC, T], BF16)
        self.Ar = [Reg(f"A{k}") for k in range(FC)]
        self.SQ = self.ring("SQ", 4, [P, T], BF16)
        self.RS = self.ring("RS", 4, [P, T], F32)
        self.SG = self.ring("SG", 2, [P, T], F32)
        if first:
            XIN = [self.psb(f"XIN{i}", [P, 4, D], F32) for i in range(2)]
            XINr = [Reg(f"XIN{i}") for i in range(2)]
        if post and full:
            G = self.psb("G", [P, 16, T], BF16)
            Gr = [Reg(f"G{k}") for k in range(16)]
            M = self.psb("M", [P, KC, T], BF16)
            Mr = [Reg(f"M{k}") for k in range(KC)]
            OF = [self.psb(f"OF{i}", [P, 8, T], BF16) for i in range(2)]
            OFr = [Reg(f"OF{i}") for i in range(2)]
        if (post and full) or (pre and full):
            TMP = self.ring("TMP", 4, [P, T], F32)
        if pre and full:
            QK = self.psb("QK", [P, 8, T], BF16)
            QKr = [Reg(f"QK{k}") for k in range(8)]
            U = self.psb("U", [P, 4, T], BF16)
            Ur = [Reg(f"U{k}") for k in range(4)]
            V = self.psb("V", [P, 4, T], BF16)
            Vr = [Reg(f"V{k}") for k in range(4)]
            CS = [self.psb(f"CS{i}", [P, 2, T], F32) for i in range(2)]
            CSr = [Reg(f"CS{i}") for i in range(2)]
        if last:
            YO = self.ring("YO", 2, [P, D], F32)

        def tok(t):
            return slice(t * T, (t + 1) * T)

        def issue_loads(t):
            b = t % 2
            if first:
                S.dma("sp", [(XIN[b][:], self.x_in[tok(t), :].rearrange("(s p) d -> p s d", p=P))], XINr[b], True)
            else:
                S.dma("sp", [(X[b][:], self.xT[:, :, tok(t)].rearrange("k p n -> p k n"))], Xr[b][0], True,
                      writes=Xr[b][1:])
            if post and full:
                S.dma("sp", [(OF[b][:, 0:4, :], self.oT[:, :, tok(t)].rearrange("k p n -> p k n")),
                             (OF[b][:, 4:8, :], self.fT[:, :, tok(t)].rearrange("k p n -> p k n"))], OFr[b], True)
            if pre and full:
                S.dma("sp", [(CS[b][:], self.rope[:, :, tok(t)].rearrange("k p n -> p k n"))], CSr[b], True)

        names = []
        if post and full:
            names.append("mix")
        if post:
            names.append("ffn2")
        if pre:
            names.append("ffn1")
        if pre and full:
            names.append("proj")
        if last:
            names.append("final")
        pieces = {
            "mix": ([("w_gate", lp, i) for i in range(4)] + [("w_papf", lp, i) for i in range(2)] + [("w_o", lp, i) for i in range(2)]) if post else [],
            "ffn2": ([("w_up", 2 * lp + 1, i) for i in range(11)] + [("w_down", 2 * lp + 1, j) for j in range(8)]) if post else [],
            "ffn1": ([("w_up", 2 * ln, i) for i in range(11)] + [("w_down", 2 * ln, j) for j in range(8)]) if pre else [],
            "proj": [("w_inp", ln, i) for i in range(6)] if pre else [],
            "final": [],
        }
        plan = []
        for pr_ in range(NT // 2):
            for nm in names:
                plan += pieces[nm] + pieces[nm]
        self.W = self.WRing(self, 5, plan)
        W = self.W
        gsel = {"mix": gcol(lp, 1), "ffn2": gcol(lp, 2), "ffn1": gcol(ln, 0), "proj": gcol(ln, 1), "final": gfin}

        def Nstage(t, nm):
            b = t % 2
            Xt, Xtr = X[b], Xr[b]
            if first and nm == names[0]:
                for k in range(KC):
                    bank, br = self.ps_next()
                    for s_ in range(4):
                        S.run("pe", lambda e, bank=bank, s_=s_, k=k: e.transpose(
                            bank[:, s_ * P:(s_ + 1) * P], XIN[b][:, s_, k * P:(k + 1) * P], self.ident[:]),
                            reads=[XINr[b], self.identr], writes=([br] if s_ in (0, 3) else []))
                    S.run("act", lambda e, bank=bank, k=k: e.activation(out=Xt[:, k, :], in_=bank, func=AF.Copy),
                          reads=[br], writes=[Xtr[k]])
            if nm == "final":
                self.norm(Xt, Xtr, gsel[nm], Xt, Xtr)
            else:
                self.norm(Xt, Xtr, gsel[nm], Hs[b], Hrs[b])

        def Mstage(t, nm):
            b = t % 2
            Xt, Xtr = X[b], Xr[b]
            H, Hr = Hs[b], Hrs[b]
            if nm == "mix":
                for i in range(4):
                    Wt, Wr = W.next(("w_gate", lp, i))
                    for c in range(4):
                        j = 4 * i + c
                        bk, bkr = self.ps_next()
                        self.mm(bk, bkr, [(Wt[:, k * 512 + c * 128:k * 512 + c * 128 + 128], H[:, k, :]) for k in range(KC)],
                                reads=[Wr] + Hr)
                        S.run("act", lambda e, bk=bk, j=j: e.activation(out=G[:, j, :], in_=bk, func=AF.Sigmoid),
                              reads=[bkr], writes=[Gr[j]])
                Wpa, Wpar = W.next(("w_papf", lp, 0))
                Wpf, Wpfr = W.next(("w_papf", lp, 1))
                OFt = OF[b]
                for j in range(KC):
                    ba, bar = self.ps_next()
                    self.mm(ba, bar, [(Wpa[:, k * 1024 + j * 128:k * 1024 + j * 128 + 128], OFt[:, k, :]) for k in range(4)],
                            reads=[Wpar, OFr[b]])
                    bf, bfr = self.ps_next()
                    self.mm(bf, bfr, [(Wpf[:, k * 1024 + j * 128:k * 1024 + j * 128 + 128], OFt[:, 4 + k, :]) for k in range(4)],
                            reads=[Wpfr, OFr[b]])
                    t1, t1r = TMP()
                    t2, t2r = TMP()
                    S.run("dve", lambda e, ba=ba, t1=t1, j=j: e.tensor_tensor(out=t1[:], in0=ba, in1=G[:, j, :], op=ALU.mult),
                          reads=[bar, Gr[j]], writes=[t1r])
                    S.run("dve", lambda e, bf=bf, t2=t2, j=j: e.tensor_tensor(out=t2[:], in0=bf, in1=G[:, 8 + j, :], op=ALU.mult),
                          reads=[bfr, Gr[8 + j]], writes=[t2r])
                    S.run("dve", lambda e, t1=t1, t2=t2, j=j: e.tensor_tensor(out=M[:, j, :], in0=t1[:], in1=t2[:], op=ALU.add),
                          reads=[t1r, t2r], writes=[Mr[j]])
                for i in range(2):
                    Wo, Wor = W.next(("w_o", lp, i))
                    for c in range(4):
                        j = 4 * i + c
                        bk, bkr = self.ps_next()
                        self.mm(bk, bkr, [(Wo[:, k * 512 + c * 128:k * 512 + c * 128 + 128], M[:, k, :]) for k in range(KC)],
                                reads=[Wor] + Mr)
                        S.run("dve", lambda e, bk=bk, j=j: e.tensor_tensor(out=Xt[:, j, :], in0=bk, in1=Xt[:, j, :], op=ALU.add),
                              reads=[bkr], writes=[Xtr[j]])
            elif nm == "ffn2":
                self.ffn(Xt, Xtr, gsel[nm], 2 * lp + 1, H, Hr, prenormed=True)
            elif nm == "ffn1":
                self.ffn(Xt, Xtr, gsel[nm], 2 * ln, H, Hr, prenormed=True)
            elif nm == "proj":
                CSt, CStr = CS[b], CSr[b]
                for qk in range(2):
                    Wa, War = W.next(("w_inp", ln, 2 * qk))
                    Wb, Wbr = W.next(("w_inp", ln, 2 * qk + 1))
                    for c in range(4):
                        b1, b1r = self.ps_next()
                        self.mm(b1, b1r, [(Wa[:, k * 512 + c * 128:k * 512 + c * 128 + 128], H[:, k, :]) for k in range(KC)],
                                reads=[War] + Hr)
                        b2, b2r = self.ps_next()
                        self.mm(b2, b2r, [(Wb[:, k * 512 + c * 128:k * 512 + c * 128 + 128], H[:, k, :]) for k in range(KC)],
                                reads=[Wbr] + Hr)
                        t1, t1r = TMP()
                        t2, t2r = TMP()
                        S.run("dve", lambda e, b1=b1, t1=t1: e.tensor_tensor(out=t1[:], in0=b1, in1=CSt[:, 0, :], op=ALU.mult),
                              reads=[b1r, CStr], writes=[t1r])
                        S.run("dve", lambda e, b2=b2, t2=t2: e.tensor_tensor(out=t2[:], in0=b2, in1=CSt[:, 1, :], op=ALU.mult),
                              reads=[b2r, CStr], writes=[t2r])
                        jj = 4 * qk + c
                        S.run("dve", lambda e, t1=t1, t2=t2, jj=jj: e.tensor_tensor(out=QK[:, jj, :], in0=t1[:], in1=t2[:], op=ALU.add),
                              reads=[t1r, t2r], writes=[QKr[jj]])
                Wu, Wur = W.next(("w_inp", ln, 4))
                for c in range(4):
                    bk, bkr = self.ps_next()
                    self.mm(bk, bkr, [(Wu[:, k * 512 + c * 128:k * 512 + c * 128 + 128], H[:, k, :]) for k in range(KC)],
                            reads=[Wur] + Hr)
                    S.run("act", lambda e, bk=bk, c=c: e.activation(out=U[:, c, :], in_=bk, func=AF.Copy),
                          reads=[bkr], writes=[Ur[c]])
                Wv, Wvr = W.next(("w_inp", ln, 5))
                for s_ in range(4):
                    bk, bkr = self.ps_next()
                    self.mm(bk, bkr, [(H[:, k, s_ * P:(s_ + 1) * P], Wv[:, k * 512:(k + 1) * 512]) for k in range(KC)],
                            reads=[Wvr] + Hr)
                    S.run("act", lambda e, bk=bk, s_=s_: e.activation(out=V[:, s_, :], in_=bk, func=AF.Copy),
                          reads=[bkr], writes=[Vr[s_]])
                S.dma("pool", [(self.qkT[:, :, tok(t)].rearrange("k p n -> p k n"), QK[:])], QKr[0], False, reads=QKr[1:])
                S.dma("pool", [(self.uT[:, :, tok(t)].rearrange("k p n -> p k n"), U[:])], Ur[0], False, reads=Ur[1:])
                S.dma("pool", [(self.vtok[tok(t), :].rearrange("(s p) e -> p s e", p=P), V[:])], Vr[0], False, reads=Vr[1:])
            elif nm == "final":
                for s_ in range(4):
                    yo, yor = YO()
                    for half in range(2):
                        bank, br = self.ps_next()
                        for kk in range(4):
                            k = half * 4 + kk
                            S.run("pe", lambda e, bank=bank, s_=s_, k=k, kk=kk: e.transpose(
                                bank[:, kk * P:(kk + 1) * P], Xt[:, k, s_ * P:(s_ + 1) * P], self.ident[:]),
                                reads=[Xtr[k], self.identr], writes=([br] if kk in (0, 3) else []))
                        S.run("act", lambda e, bank=bank, yo=yo, half=half: e.activation(
                            out=yo[:, half * 512:(half + 1) * 512], in_=bank, func=AF.Copy),
                            reads=[br], writes=[yor])
                    S.dma("pool", [(self.y_out[t * T + s_ * P:t * T + (s_ + 1) * P, :], yo[:])], yor, False)
            if pre and nm == names[-1]:
                S.dma("pool", [(self.xT[:, :, tok(t)].rearrange("k p n -> p k n"), Xt[:])], Xtr[0], False, reads=Xtr[1:])

        assert NT % 2 == 0
        ns = len(names)
        issue_loads(0)
        issue_loads(1)
        Nstage(0, names[0])
        Nstage(1, names[0])
        for pr_ in range(NT // 2):
            a, b_ = 2 * pr_, 2 * pr_ + 1
            for si, nm in enumerate(names):
                for t in (a, b_):
                    Mstage(t, nm)
                    if si + 1 < ns:
                        Nstage(t, names[si + 1])
                    elif t + 2 < NT:
                        issue_loads(t + 2)
                        Nstage(t + 2, names[0])

    def ps_ring(self, banks):
        state = {"i": 0}

        def nxt():
            b = banks[state["i"] % len(banks)]
            state["i"] += 1
            return self.ps[:, b, :], self.psr[b]

        return nxt

    def mix_phase(self, l):
        self.attention(l)
        self.S.barrier()
        self.fourier(l)

    def attention(self, l):
        S, CH, NTC = self.S, self.CH, self.NTC
        if l + 1 < self.L:
            self.issue_conv(l + 1)
        LA = 3
        POOL_EVERY = self.cfg.get("pool_every", 2)
        nbc = CH // P
        sp_state = {"i": 0}

        def SPAIR():
            p = sp_state["i"] % 2
            sp_state["i"] += 1
            return p, self.psr[2 * p]

        acc_pairs = [4, 6]
        KT = [self.psb(f"KT{i}", [P, 2 * CH], BF16) for i in range(2)]
        KTr = [Reg(f"KT{i}") for i in range(2)]
        VH = [self.psb(f"VH{i}", [P, 2 * nbc, P], BF16) for i in range(2)]
        VHr = [Reg(f"VH{i}") for i in range(2)]
        QT = [self.psb(f"QT{i}", [P, T], BF16) for i in range(2)]
        QTr = [Reg(f"QT{i}") for i in range(2)]
        ER = self.ring("E", LA + 3, [P, 2, T], BF16)
        ACCD = [self.psb(f"ACCD{i}", [P, 2, T], F32) for i in range(2)]
        ACCDr = [Reg("accd") for i in range(2)]
        ACCP = [self.psb(f"ACCP{i}", [P, 2, T], F32) for i in range(2)]
        ACCPr = [Reg("accp") for i in range(2)]
        ACCB = self.ring("ACCB", 2, [P, 2, T], BF16)
        RR = self.ring("RR", 2, [P, 2, T], F32)
        TT = self.ring("TT", 2, [P, 2, T], F32)
        OB = self.ring("OB", 2, [P, T], F32)
        SQ = self.ring("ASQ", 2, [P, T], BF16)
        RS = self.ring("ARS", 4, [P, T], F32)
        OS = self.ring("OS", 2, [P, T], BF16)

        groups = [(0, 2 * CH), (2 * CH, CH)]
        heads = []
        for (g0, nk) in groups:
            for h in range(NH):
                heads.append((g0, nk, h))
        items = []
        qunits = []
        for hi, (g0, nk, h) in enumerate(heads):
            for qt in range(nk // T):
                qi = len(qunits)
                qunits.append((hi, g0 + qt * T))
                nkb = nk // P
                for kb in range(nkb):
                    items.append(dict(hi=hi, qi=qi, kb=kb, nkb=nkb, h=h, g0=g0, q0=g0 + qt * T,
                                      first_h=(qt == 0 and kb == 0), first_q=(kb == 0)))

        def load_head(hi):
            g0, nk, h = heads[hi]
            b = hi % 2
            S.dma("sp", [(KT[b][:, 0:nk], self.qkT[4 + h, :, g0:g0 + nk])], KTr[b], True)
            S.dma("sp", [(VH[b][:, 0:nk // P, :], self.vtok[g0:g0 + nk, h * P:(h + 1) * P].rearrange("(kb p) e -> p kb e", p=P))],
                  VHr[b], True)

        def load_q(qi):
            hi, q0 = qunits[qi]
            h = heads[hi][2]
            S.dma("sp", [(QT[qi % 2][:], self.qkT[h, :, q0:q0 + T])], QTr[qi % 2], True)

        deferred = []

        def stage1(it):
            hb, qb = it["hi"] % 2, it["qi"] % 2
            kb = it["kb"]
            p, pr = SPAIR()
            E, Er = ER()
            it["E"], it["Er"] = E, Er
            S.run("pe", [lambda e, c=c: e.matmul(self.ps[:, 2 * p + c, :], KT[hb][c * 64:(c + 1) * 64, kb * P:(kb + 1) * P],
                                                 QT[qb][c * 64:(c + 1) * 64, :], start=True, stop=True) for c in range(2)],
                  reads=[KTr[hb], QTr[qb]], writes=[pr])
            cross = (it["g0"] == 0) and ((kb // nbc) != ((it["q0"] // T) // NTC))
            bias = self.xb if cross else self.zero1
            S.run("act", lambda e: e.activation(out=E[:], in_=self.ps[:, 2 * p:2 * p + 2, :], func=AF.Exp, bias=bias[:], scale=0.125),
                  reads=[pr, self.xbr, self.zero1r], writes=[Er])

        def stage2(it):
            hb = it["hi"] % 2
            kb, nkb = it["kb"], it["nkb"]
            par = it["qi"] % 2
            ob, sbk = 4, 6
            Or, Sr = self.psr[ob], self.psr[sbk]
            E, Er = it["E"], it["Er"]
            S.run("pe", [lambda e, c=c: e.matmul(self.ps[:, ob + c, :], VH[hb][:, kb, :], E[:, c, :], start=(kb == 0), stop=(kb == nkb - 1))
                         for c in range(2)],
                  reads=[VHr[hb], Er], writes=([Or] if kb in (0, nkb - 1) else []))
            m = kb % 4
            if m in (1, 3):
                S.run("pe", [lambda e, c=c: e.matmul(self.ps[:, sbk + c, :], self.ones[:], E[:, c, :], start=(kb == 1), stop=False)
                             for c in range(2)],
                      reads=[Er, self.onesr], writes=([Sr] if kb == 1 else []))
            else:
                use_pool = (m == 2)
                eng = "pool" if use_pool else "dve"
                acc, accr = (ACCP[par], ACCPr[par]) if use_pool else (ACCD[par], ACCDr[par])
                if kb < 4:
                    S.run(eng, lambda e: e.tensor_copy(out=acc[:], in_=E[:]), reads=[Er], writes=[accr])
                else:
                    S.run(eng, lambda e: e.tensor_tensor(out=acc[:], in0=acc[:], in1=E[:], op=ALU.add), reads=[Er], writes=[accr])
            if kb == nkb - 1:
                h, q0 = it["h"], it["q0"]
                ab, abr = ACCB()
                S.run("dve", lambda e: e.tensor_tensor(out=ab[:], in0=ACCD[par][:], in1=ACCP[par][:], op=ALU.add),
                      reads=[ACCDr[par], ACCPr[par]], writes=[abr])
                S.run("pe", [lambda e, c=c: e.matmul(self.ps[:, sbk + c, :], self.ones[:], ab[:, c, :], start=False, stop=True)
                             for c in range(2)], reads=[abr, self.onesr], writes=[Sr])
                R, Rr = RR()
                S.run("dve", lambda e: e.reciprocal(out=R[:], in_=self.ps[:, sbk:sbk + 2, :]), reads=[Sr], writes=[Rr])
                tt, ttr = TT()
                S.run("dve", lambda e: e.tensor_tensor(out=tt[:], in0=self.ps[:, ob:ob + 2, :], in1=R[:], op=ALU.mult),
                      reads=[Or, Rr], writes=[ttr])

                def fin():
                    o, orr = OB()
                    S.run("dve", lambda e: e.scalar_tensor_tensor(out=o[:], in0=tt[:, 1, :], scalar=self.neglam[:, l:l + 1], in1=tt[:, 0, :],
                                                                  op0=ALU.mult, op1=ALU.add),
                          reads=[ttr, self.scalr], writes=[orr])
                    sq, sqr = SQ()
                    S.run("act", lambda e: e.activation(out=sq[:], in_=o[:], func=AF.Square), reads=[orr], writes=[sqr])
                    p2, p2r = SPAIR()
                    bk = self.ps[:, 2 * p2, :]
                    S.run("pe", lambda e: e.matmul(bk, self.ones[:], sq[:], start=True, stop=True), reads=[sqr, self.onesr], writes=[p2r])
                    r0, r0r = RS()
                    r1, r1r = RS()
                    S.run("act", lambda e: e.activation(out=r0[:], in_=bk, func=AF.Sqrt, bias=self.epst[:], scale=1.0 / P),
                          reads=[p2r, self.epsr], writes=[r0r])
                    S.run("dve", lambda e: e.reciprocal(out=r1[:], in_=r0[:]), reads=[r0r], writes=[r1r])
                    os_, osr = OS()
                    S.run("dve", lambda e: e.scalar_tensor_tensor(out=os_[:], in0=o[:], scalar=self.gs[:, l:l + 1], in1=r1[:],
                                                                  op0=ALU.mult, op1=ALU.mult),
                          reads=[orr, r1r, self.scalr], writes=[osr])
                    S.dma("pool", [(self.oT[h, :, q0:q0 + T], os_[:])], osr, False)

                deferred.append([5, fin])

        n = len(items)
        load_head(0)
        load_q(0)
        for idx in range(n + LA):
            if idx < n:
                stage1(items[idx])
            if idx >= LA:
                it = items[idx - LA]
                stage2(it)
                if it["first_h"] and it["hi"] + 1 < len(heads):
                    load_head(it["hi"] + 1)
                if it["first_q"] and it["qi"] + 1 < len(qunits):
                    load_q(it["qi"] + 1)
            for d in deferred:
                d[0] -= 1
            ready = [d for d in deferred if d[0] <= 0]
            for d in ready:
                deferred.remove(d)
                d[1]()
        while deferred:
            deferred.pop(0)[1]()

    def fourier(self, l):
        S, CH = self.S, self.CH
        nb = CH // P
        half = nb // 2
        hc = CH // 2
        PB = self.ps_ring([0, 1, 2, 3, 4, 5, 6, 7])
        UT = [self.psb(f"UT{i}", [P, self.NTOK], BF16) for i in range(2)]
        UTr = [Reg(f"UT{i}") for i in range(2)]
        PQ = [self.psb(f"PQ{g}", [P, nb, 256], BF16) for g in range(4)]
        PQr = [[Reg(f"PQ{g}_{i}") for i in range(nb // 2)] for g in range(4)]
        MT = [[self.psb(f"MT{i}_{cs}", [P, nb, T], BF16) for cs in range(2)] for i in range(2)]
        MTr = [Reg(f"MT{i}") for i in range(2)]
        FS = self.ring("FS", 3, [P, T], BF16)
        ui = 0
        mi = 0
        for X in range(3):
            for g in range(4):
                ub = ui % 2
                ui += 1
                S.dma("sp", [(UT[ub][:], self.uT[g, :, :])], UTr[ub], True)
                for rp in range(nb // 2):
                    bank, br = PB()
                    for sub in range(2):
                        rb = 2 * rp + sub
                        if X == 2:
                            terms = [(2 * CH + rb * P, 0)]
                        else:
                            j0 = (rb % half) * P
                            if X == 0:
                                terms = [(j0, 0), (hc + j0, 1)] if rb < half else [(hc + j0, 2), (CH + j0, 1), (CH + hc + j0, 1)]
                            else:
                                terms = [(j0, 1), (hc + j0, 3), (CH + j0, 2)] if rb < half else [(CH + j0, 1), (CH + hc + j0, 4)]
                        nt = len(terms)
                        for ti, (tk0, ci) in enumerate(terms):
                            tracked = (sub == 0 and ti == 0) or (sub == 1 and ti == nt - 1)
                            S.run("pe", lambda e, bank=bank, sub=sub, tk0=tk0, ci=ci, ti=ti, nt=nt, ub=ub: e.matmul(
                                bank[:, sub * 256:(sub + 1) * 256], UT[ub][:, tk0:tk0 + P], self.cs5t[:, ci, :],
                                start=(ti == 0), stop=(ti == nt - 1)),
                                reads=[UTr[ub], self.cs5r], writes=([br] if tracked else []))
                    S.run("act", lambda e, bank=bank, g=g, rp=rp: e.activation(
                        out=PQ[g][:, 2 * rp:2 * rp + 2, :], in_=bank.rearrange("p (a b) -> p a b", a=2), func=AF.Copy),
                        reads=[br], writes=[PQr[g][rp]])
            for ct in range(CH // T):
                mb = mi % 2
                mi += 1
                S.dma("sp", [(MT[mb][cs][:], self.mtab[X, cs, :, ct * T:(ct + 1) * T].rearrange("(b p) n -> p b n", p=P))
                             for cs in range(2)], MTr[mb], True)
                for g in range(4):
                    bank, br = PB()
                    pairs = []
                    for blk in range(nb):
                        pairs.append((PQ[g][:, blk, 0:P], MT[mb][0][:, blk, :]))
                        pairs.append((PQ[g][:, blk, P:2 * P], MT[mb][1][:, blk, :]))
                    self.mm(bank, br, pairs, reads=[MTr[mb]] + PQr[g])
                    fs, fsr = FS()
                    S.run("dve", lambda e, bank=bank, fs=fs: e.tensor_copy(out=fs[:], in_=bank), reads=[br], writes=[fsr])
                    S.dma("pool", [(self.fT[g, :, X * CH + ct * T:X * CH + (ct + 1) * T], fs[:])], fsr, False)


def _assignment(n_link, n_unl):
    cores = []
    for i in range(n_link):
        cores.append(("L", i, i))
    for j in range(n_unl):
        cores.append(("U", [n_link + 3 * j + c for c in range(3)]))
    return cores


def _slot_positions(kind, CH):
    j = np.arange(CH)
    if kind == "L":
        return np.concatenate([2 * j, 2 * j + 1, j])
    return np.concatenate([j, j, j])


def _rope_table(pos):
    inv = (1.0 / (10000.0 ** (np.arange(0, 64, 2, dtype=np.float32) / np.float32(64)))).astype(np.float32)
    ang = pos.astype(np.float32)[:, None] * inv[None, :]
    ang = np.concatenate([ang, ang], -1)
    c = np.cos(ang).astype(np.float32)
    s = np.sin(ang).astype(np.float32)
    sign = np.where(np.arange(64) < 32, -1.0, 1.0).astype(np.float32)
    s = s * sign[None, :]
    c2 = np.concatenate([c, c], -1).T
    s2 = np.concatenate([s, s], -1).T
    return np.ascontiguousarray(np.stack([c2, s2], 0))


def _dft_tables(kind, CH):
    r = np.arange(CH, dtype=np.int64)
    t = np.arange(CH, dtype=np.int64)

    def tab(mk, S, norm):
        ang = 2.0 * np.pi * (mk % S).astype(np.float64) / S
        return np.stack([np.cos(ang) * norm, -np.sin(ang) * norm], 0)

    nC = 1.0 / math.sqrt(CH * 128.0)
    MC = tab(r[:, None] * t[None, :], CH, nC)
    if kind == "U":
        out = np.stack([MC, MC, MC], 0)
    else:
        S2 = 2 * CH
        h = CH // 2
        m = np.where(r < h, 2 * r, 2 * (r - h) + 1)
        nL = 1.0 / math.sqrt(S2 * 128.0)
        MA = tab(m[:, None] * (2 * t[None, :]), S2, nL)
        MB = tab(m[:, None] * (2 * t[None, :] + 1), S2, nL)
        out = np.stack([MA, MB, MC], 0)
    return out.astype(np.float32).astype(ml_dtypes.bfloat16)


def _cs5(kind):
    f = 1.0 if kind == "L" else 0.0
    c = np.arange(128, dtype=np.int64)
    ang = 2.0 * np.pi * ((c[:, None] * c[None, :]) % 128).astype(np.float64) / 128.0
    CSm = np.concatenate([np.cos(ang), np.sin(ang)], 1)
    coefs = [1.0, f, 1.0 - f, -f, 1.0 - 2.0 * f]
    return np.stack([CSm * cf for cf in coefs], 0).astype(np.float32)


def make_in_maps(cfg, inp, cores):
    L, CH = cfg["depth"], cfg["ch"]
    xp, xs = np.asarray(inp["x_prompt"]), np.asarray(inp["x_sample"])
    shared = _layout_weights(inp, L)
    gcols = []
    for l in range(L):
        for nm in ("g_ff1", "g_mix", "g_ff2"):
            gcols.append(np.asarray(inp[nm])[l].reshape(KC, P).T)
    gcols.append(np.asarray(inp["g_final"]).reshape(KC, P).T)
    shared["gvec"] = np.ascontiguousarray(np.concatenate(gcols, 1))
    shared["gsub"] = np.ascontiguousarray(np.asarray(inp["g_sub"])[:L].T)
    shared["lamv"] = np.ascontiguousarray(np.stack([np.asarray(inp[k])[:L] for k in ("lam_q1", "lam_k1", "lam_q2", "lam_k2")], 0))
    shared["ident_in"] = np.eye(P, dtype=np.float32)
    per_kind = {}
    for kind in ("L", "U"):
        per_kind[kind] = dict(
            rope=_rope_table(_slot_positions(kind, CH)),
            xbias=np.full((P, 1), 0.0 if kind == "L" else NEG, np.float32),
            cs5=_cs5(kind),
            mtab=_dft_tables(kind, CH),
        )
    maps = []
    for c in cores:
        m = dict(shared)
        m.update(per_kind[c[0]])
        if c[0] == "L":
            x = np.concatenate([xs[c[1]][0::2], xs[c[1]][1::2], xp[c[2]]], 0)
        else:
            x = np.concatenate([xp[i] for i in c[1]], 0)
        m["x_in"] = np.ascontiguousarray(x)
        maps.append(m)
    return maps


def gather_outputs(cfg, results, cores, n_prompt, n_sample):
    CH = cfg["ch"]
    yp = np.zeros((n_prompt, CH, D), np.float32)
    ys = np.zeros((n_sample, 2 * CH, D), np.float32)
    for c, r in zip(cores, results):
        y = r["y_out"]
        if c[0] == "L":
            ys[c[1]][0::2] = y[0:CH]
            ys[c[1]][1::2] = y[CH:2 * CH]
            yp[c[2]] = y[2 * CH:3 * CH]
        else:
            for i, pi in enumerate(c[1]):
                yp[pi] = y[i * CH:(i + 1) * CH]
    return yp, ys


def kernel(**inputs):
    cfg = dict(FULL_CFG)
    cores = _assignment(4, 4)
    nc = Builder(cfg).build()
    maps = make_in_maps(cfg, inputs, cores)
    res = run_bass_kernel_spmd(nc, maps, core_ids=list(range(8)))
    yp, ys = gather_outputs(cfg, res.results, cores, 16, 4)
    return (yp, ys)
```

```python
import math
from contextlib import ExitStack

import numpy as np
import ml_dtypes

import concourse.bass as bass
import concourse.mybir as mybir
from concourse.bass_utils import run_bass_kernel_spmd

F32 = mybir.dt.float32
BF16 = mybir.dt.bfloat16
AF = mybir.ActivationFunctionType
ALU = mybir.AluOpType
AX = mybir.AxisListType

P = 128
D = 1024
KC = 8
DFF = 2816
FC = 22
T = 512
NH = 4
EPS = 1e-6
SEM_LIMIT = 30000
NEG = -30000.0

FULL_CFG = dict(depth=4, ch=2048)


class Reg:
    __slots__ = ("w", "r", "lsem", "ssem", "name")

    def __init__(self, name=""):
        self.w = None
        self.r = []
        self.lsem = None
        self.ssem = None
        self.name = name


class Sched:
    ENGS = ("pe", "act", "dve", "pool", "sp")

    def __init__(self, nc, es):
        self.nc = nc
        self.es = es
        self.q = {e: [] for e in self.ENGS}
        self.cur = {}
        self.waited = {e: {} for e in self.ENGS}
        self.nsem = 0
        self.epoch = 0
        self.active = []
        self.free_holders = []
        self.semobjs = []

    def new_sem(self):
        self.nsem += 1
        s = self.es.enter_context(self.nc.semaphore(f"s{self.nsem}"))
        self.semobjs.append(s)
        return (self.nsem, s)

    def _filter(self, eng, waits):
        wl = []
        w = self.waited[eng]
        for tk in waits:
            if tk is None:
                continue
            sem, val, ep = tk
            if ep < self.epoch:
                continue
            k = sem[0]
            if w.get(k, 0) >= val:
                continue
            w[k] = val
            wl.append((sem[1], val))
        return wl

    def op(self, eng, fns, waits=(), tok=None, signal=True):
        if callable(fns):
            fns = [fns]
        wl = self._filter(eng, waits)
        kind = "eng"
        if tok is not None:
            kind = "dma"
        elif signal:
            c = self.cur.get(eng)
            if c is None or c[1] >= SEM_LIMIT:
                c = [self.new_sem(), 0]
                self.cur[eng] = c
            c[1] += 1
            tok = (c[0], c[1], self.epoch)
        self.q[eng].append((wl, fns, tok, kind))
        return tok

    def run(self, eng, fns, reads=(), writes=(), extra=()):
        waits = list(extra)
        for g in reads:
            waits.append(g.w)
        for g in writes:
            waits.append(g.w)
            waits.extend(g.r)
        tok = self.op(eng, fns, waits)
        for g in reads:
            g.r.append(tok)
        for g in writes:
            g.w = tok
            g.r = []
        return tok

    def _holder(self, old, longlived):
        if old is not None and (old["long"] or old["epoch"] == self.epoch):
            return old
        if longlived:
            return dict(sem=self.new_sem(), count=0, epoch=self.epoch, long=True)
        if self.free_holders:
            h = self.free_holders.pop()
            h["epoch"] = self.epoch
        else:
            h = dict(sem=self.new_sem(), count=0, epoch=self.epoch, long=False)
        self.active.append(h)
        return h

    def dma(self, qeng, outs_ins, sb_reg, load, reads=(), writes=(), longlived=False):
        if load:
            holder = sb_reg.lsem = self._holder(sb_reg.lsem, longlived)
        else:
            holder = sb_reg.ssem = self._holder(sb_reg.ssem, longlived)
        waits = []
        rds = list(reads) + ([] if load else [sb_reg])
        wrs = list(writes) + ([sb_reg] if load else [])
        for g in rds:
            waits.append(g.w)
        for g in wrs:
            waits.append(g.w)
            waits.extend(g.r)
        ep = (1 << 60) if holder["long"] else self.epoch
        waits.append((holder["sem"], holder["count"] * 16, ep))
        fns = []
        for (o, i) in outs_ins:
            fns.append(lambda e, o=o, i=i: e.dma_start(out=o, in_=i))
        holder["count"] += len(fns)
        tok = (holder["sem"], holder["count"] * 16, ep)
        self.op(qeng, fns, waits, tok=tok)
        for g in rds:
            g.r.append(tok)
        for g in wrs:
            g.w = tok
            g.r = []
        return tok

    def barrier(self):
        toks = []
        for e, c in self.cur.items():
            toks.append((c[0], c[1], self.epoch))
        for h in self.active:
            if h["count"] > 0:
                toks.append((h["sem"], h["count"] * 16, self.epoch))
        for e in self.ENGS:
            self.op(e, [], toks, signal=False)

    def end_phase(self):
        self.free_holders.extend(self.active)
        self.active = []
        self.epoch += 1

    def emit(self):
        nc = self.nc
        with nc.Block() as block:

            def mk(name):
                def body(e):
                    for wl, fns, tok, kind in self.q[name]:
                        for sem, val in wl:
                            e.wait_ge(sem, val)
                        n = len(fns)
                        for i, f in enumerate(fns):
                            ins = f(e)
                            if kind == "dma":
                                ins.then_inc(tok[0][1], 16)
                            elif tok is not None and i == n - 1:
                                ins.then_inc(tok[0][1], 1)
                return body

            block.tensor(mk("pe"))
            block.scalar(mk("act"))
            block.vector(mk("dve"))
            block.gpsimd(mk("pool"))
            block.sync(mk("sp"))
        self.q = {e: [] for e in self.ENGS}


def _pieces(W, col_groups):
    K = W.shape[0]
    kc = K // P
    out = []
    for cols in col_groups:
        sub = W[:, cols]
        sub = sub.reshape(kc, P, len(cols)).transpose(1, 0, 2).reshape(P, kc * len(cols))
        out.append(sub)
    return np.ascontiguousarray(np.stack(out, 0))


def _rot_perm():
    idx = np.arange(512)
    g = idx // 64
    d = idx % 64
    return g * 64 + (d + 32) % 64


def _layout_weights(inp, L):
    up_groups = []
    for i in range(11):
        cols = np.concatenate([np.arange(256 * i, 256 * i + 256), DFF + np.arange(256 * i, 256 * i + 256)])
        up_groups.append(cols)
    down_groups = [np.arange(128 * j, 128 * j + 128) for j in range(8)]
    w_up, w_down, w_inp, w_gate, w_papf, w_o = [], [], [], [], [], []
    rp = _rot_perm()
    for l in range(L):
        for nm in ("ff1", "ff2"):
            w_up.append(_pieces(inp[f"w_{nm}_up"][l], up_groups))
            w_down.append(_pieces(inp[f"w_{nm}_down"][l], down_groups))
        wi = inp["w_in"][l]
        groups = [np.arange(0, 512), rp, 512 + np.arange(512), 512 + rp, 1536 + np.arange(512), 1024 + np.arange(512)]
        w_inp.append(_pieces(wi, groups))
        w_gate.append(_pieces(inp["w_gate"][l], [np.arange(512 * i, 512 * i + 512) for i in range(4)]))
        w_papf.append(np.concatenate([_pieces(inp["w_pa"][l], [np.arange(1024)]), _pieces(inp["w_pf"][l], [np.arange(1024)])], 0))
        w_o.append(_pieces(inp["w_o"][l], [np.arange(512 * i, 512 * i + 512) for i in range(2)]))
    return dict(
        w_up=np.stack(w_up, 0), w_down=np.stack(w_down, 0), w_inp=np.stack(w_inp, 0),
        w_gate=np.stack(w_gate, 0), w_papf=np.stack(w_papf, 0), w_o=np.stack(w_o, 0),
    )


class Builder:
    def __init__(self, cfg):
        self.cfg = cfg
        self.L = cfg["depth"]
        self.CH = cfg["ch"]
        self.NTC = self.CH // T
        self.NT = 3 * self.NTC
        self.NTOK = 3 * self.CH
        self.mode = cfg.get("mode", "full")

    def sb(self, name, shape, dt):
        return self.es.enter_context(self.nc.sbuf_tensor(name, list(shape), dt))

    def psb(self, name, shape, dt):
        self.uid += 1
        return self.pes.enter_context(self.nc.sbuf_tensor(f"{name}_{self.uid}", list(shape), dt))

    def phase(self, fn):
        with ExitStack() as pes:
            self.pes = pes
            fn()
            self.S.barrier()
            self.S.emit()
            self.S.end_phase()

    def ps_next(self):
        b = self.ps_i % 8
        self.ps_i += 1
        return self.ps[:, b, :], self.psr[b]

    def mm(self, out_ap, bank_reg, pairs, reads):
        n = len(pairs)
        fns = [
            (lambda e, l=l, r=r, i=i: e.matmul(out_ap, l, r, start=(i == 0), stop=(i == n - 1)))
            for i, (l, r) in enumerate(pairs)
        ]
        return self.S.run("pe", fns, reads=reads, writes=[bank_reg])

    def build(self):
        cfg = self.cfg
        L, CH, NT, NTOK = self.L, self.CH, self.NT, self.NTOK
        nc = bass.Bass("TRN2", target_bir_lowering=False)
        self.nc = nc

        def din(name, shape, dt=F32):
            return nc.dram_tensor(name, list(shape), dt, kind="ExternalInput").ap()

        def dscr(name, shape, dt):
            kind = "ExternalOutput" if (cfg.get("debug") and name in ("qkT", "vtok", "uT", "oT", "fT", "xT")) else "Internal"
            return nc.dram_tensor(name, list(shape), dt, kind=kind).ap()

        self.x_in = din("x_in", [NTOK, D])
        self.wf = dict(
            w_up=din("w_up", [2 * L, 11, P, 4096]), w_down=din("w_down", [2 * L, 8, P, 2816]),
            w_inp=din("w_inp", [L, 6, P, 4096]), w_gate=din("w_gate", [L, 4, P, 4096]),
            w_papf=din("w_papf", [L, 2, P, 4096]), w_o=din("w_o", [L, 2, P, 4096]),
        )
        self.gvec = din("gvec", [P, (3 * L + 1) * 8])
        self.gsub = din("gsub", [P, L])
        self.lamv = din("lamv", [4, L, 64])
        self.rope = din("rope", [2, P, NTOK])
        self.xbias = din("xbias", [P, 1])
        self.ident_in = din("ident_in", [P, P])
        self.perm_in = din("perm_in", [P, P])
        self.cs5 = din("cs5", [5, P, 256])
        self.mtab = din("mtab", [3, 2, CH, CH], BF16)
        self.y_out = nc.dram_tensor("y_out", [NTOK, D], F32, kind="ExternalOutput").ap()

        self.wb = {k: dscr("b_" + k, v.shape, BF16) for k, v in self.wf.items()}
        self.xT = dscr("xT", [KC, P, NTOK], F32)
        self.qkT = dscr("qkT", [8, P, NTOK], BF16)
        self.vtok = dscr("vtok", [NTOK, 512], BF16)
        self.uT = dscr("uT", [4, P, NTOK], BF16)
        self.oT = dscr("oT", [4, P, NTOK], BF16)
        self.fT = dscr("fT", [4, P, NTOK], BF16)

        with ExitStack() as es:
            self.es = es
            S = Sched(nc, es)
            self.S = S
            self.ps = es.enter_context(nc.psum_tensor("ps", [P, 8, 512], F32))
            self.psr = [Reg(f"ps{b}") for b in range(8)]
            self.ps_i = 0
            self.uid = 0
            self.phase(lambda: (self.setup_consts(), self.convert_weights()))
            for p in range(L + 1):
                self.phase(lambda p=p: self.row_pass(p))
                if p < L and self.mode == "full":
                    self.phase(lambda p=p: self.attention(p))
                    self.phase(lambda p=p: self.fourier(p))
        return nc

    def setup_consts(self):
        S, nc, L = self.S, self.nc, self.L
        L = self.L
        ng = (3 * L + 1) * 8
        self.g32 = self.sb("g32", [P, ng], F32)
        self.g32r = Reg("g32")
        self.ones = self.sb("ones", [P, P], BF16)
        self.onesr = Reg("ones")
        self.ident = self.sb("ident", [P, P], F32)
        self.identr = Reg("ident")
        self.zero1 = self.sb("zero1", [P, 1], F32)
        self.zero1r = Reg("zero1")
        S.dma("sp", [(self.g32[:], self.gvec[:, :])], self.g32r, True)
        self.epst = self.sb("epst", [P, 1], F32)
        self.epsr = Reg("eps")
        S.run("pool", lambda e: e.memset(self.epst[:], EPS), writes=[self.epsr])
        S.run("pool", lambda e: e.memset(self.ones[:], 1.0), writes=[self.onesr])
        S.run("pool", lambda e: e.memset(self.zero1[:], 0.0), writes=[self.zero1r])
        S.dma("sp", [(self.ident[:], self.ident_in[:, :])], self.identr, True)
        self.permT = self.sb("permT", [P, P], BF16)
        self.permr = Reg("permT")
        S.dma("pool", [(self.permT[:], self.perm_in[:, :])], self.permr, True)
        self.neglam = self.sb("neglam", [P, L], F32)
        self.gs = self.sb("gs", [P, L], F32)
        self.scalr = Reg("scal")
        ssum = self.sb("lss", [P, 2 * L], F32)
        esum = self.sb("les", [P, 2 * L], F32)
        self.xb = self.sb("xb", [P, 1], F32)
        self.cs5t = self.sb("cs5t", [P, 5, 256], BF16)
        lv = self.psb("lv", [P, 4 * L * 64], F32)
        lvr = Reg("lv")
        S.dma("sp", [(lv[:], self.lamv.rearrange("a l d -> (a l d)").partition_broadcast(P))], lvr, True)
        S.dma("sp", [(self.gs[:], self.gsub[:, :])], self.scalr, True)
        pr = self.psb("lpr", [P, 64], F32)
        prr, ssr, esr = Reg("pr"), Reg("ss"), Reg("es")
        for l in range(L):
            lam_init = 0.8 - 0.6 * math.exp(-0.3 * l)
            for a in range(2):
                o1 = ((2 * a) * L + l) * 64
                o2 = ((2 * a + 1) * L + l) * 64
                S.run("dve", lambda e, o1=o1, o2=o2: e.tensor_tensor(out=pr[:], in0=lv[:, o1:o1 + 64], in1=lv[:, o2:o2 + 64], op=ALU.mult),
                      reads=[lvr], writes=[prr])
                S.run("dve", lambda e, a=a, l=l: e.reduce_sum(out=ssum[:, 2 * l + a:2 * l + a + 1], in_=pr[:], axis=AX.X),
                      reads=[prr], writes=[ssr])
            S.run("act", lambda e, l=l: e.activation(out=esum[:, 2 * l:2 * l + 2], in_=ssum[:, 2 * l:2 * l + 2], func=AF.Exp),
                  reads=[ssr], writes=[esr])
            S.run("dve", lambda e, l=l: e.tensor_tensor(out=self.neglam[:, l:l + 1], in0=esum[:, 2 * l + 1:2 * l + 2],
                                                        in1=esum[:, 2 * l:2 * l + 1], op=ALU.subtract),
                  reads=[esr], writes=[self.scalr])
            S.run("dve", lambda e, l=l, li=lam_init: e.tensor_scalar_add(self.neglam[:, l:l + 1], self.neglam[:, l:l + 1], -li),
                  writes=[self.scalr])
            S.run("dve", lambda e, l=l, li=lam_init: e.tensor_scalar_mul(self.gs[:, l:l + 1], self.gs[:, l:l + 1], 1.0 - li),
                  writes=[self.scalr])
        self.xbr = Reg("xb")
        S.dma("sp", [(self.xb[:], self.xbias[:, :])], self.xbr, True)
        self.cs5r = Reg("cs5")
        S.dma("pool", [(self.cs5t[:], self.cs5.rearrange("a p n -> p a n"))], self.cs5r, True)

    def convert_weights(self):
        self.conv = {}
        self.issue_conv(0)

    def conv_key(self, name, a):
        if name in ("w_up", "w_down"):
            return (a // 2, "ff1" if a % 2 == 0 else "ff2")
        return (a, "inp" if name == "w_inp" else "mix")

    def issue_conv(self, l):
        S = self.S
        groups = {"ff1": [], "inp": [], "mix": [], "ff2": []}
        for k in ("w_up", "w_down"):
            for f, gname in ((0, "ff1"), (1, "ff2")):
                for i in range(self.wf[k].shape[1]):
                    groups[gname].append((self.wb[k][2 * l + f, i], self.wf[k][2 * l + f, i]))
        for i in range(6):
            groups["inp"].append((self.wb["w_inp"][l, i], self.wf["w_inp"][l, i]))
        for k in ("w_gate", "w_papf", "w_o"):
            for i in range(self.wf[k].shape[1]):
                groups["mix"].append((self.wb[k][l, i], self.wf[k][l, i]))
        for gname in ("ff1", "inp", "mix", "ff2"):
            r = Reg(f"conv{l}{gname}")
            S.dma("pool", groups[gname], r, True, longlived=True)
            self.conv[(l, gname)] = r

    def ring(self, name, n, shape, dt):
        tiles = [self.psb(f"{name}{i}", shape, dt) for i in range(n)]
        regs = [Reg(f"{name}{i}") for i in range(n)]
        state = {"i": 0}

        def nxt():
            i = state["i"] % n
            state["i"] += 1
            return tiles[i], regs[i]

        return nxt

    class WRing:
        def __init__(self, B, nslot, plan):
            self.B = B
            self.n = nslot
            self.tiles = [B.psb(f"w{i}", [P, 4096], BF16) for i in range(nslot)]
            self.regs = [Reg(f"w{i}") for i in range(nslot)]
            self.plan = plan
            self.issued = 0
            self.used = 0

        def _issue(self, i):
            B = self.B
            name, a, b = self.plan[i]
            src = B.wb[name][a, b]
            n = src.shape[-1]
            slot = i % self.n
            B.S.dma("sp", [(self.tiles[slot][:, 0:n], src)], self.regs[slot], True, reads=[B.conv[B.conv_key(name, a)]])

        def next(self, tag):
            while self.issued < min(len(self.plan), self.used + self.n - 1):
                self._issue(self.issued)
                self.issued += 1
            i = self.used
            assert self.plan[i] == tag, (self.plan[i], tag)
            self.used += 1
            return self.tiles[i % self.n], self.regs[i % self.n]

    def norm(self, Xt, Xr, gcol, out, out_regs):
        S = self.S
        bank, br = self.ps_next()
        for k in range(KC):
            sq, sqr = self.SQ()
            S.run("act", lambda e, sq=sq, k=k: e.activation(out=sq[:], in_=Xt[:, k, :], func=AF.Square),
                  reads=[Xr[k]], writes=[sqr])
            S.run("pe", lambda e, sq=sq, k=k: e.matmul(bank, self.ones[:], sq[:], start=(k == 0), stop=(k == KC - 1)),
                  reads=[sqr, self.onesr], writes=([br] if k in (0, KC - 1) else []))
        rs0, rs0r = self.RS()
        rs, rsr = self.RS()
        S.run("act", lambda e: e.activation(out=rs0[:], in_=bank, func=AF.Sqrt, bias=self.epst[:], scale=1.0 / D),
              reads=[br, self.epsr], writes=[rs0r])
        S.run("dve", lambda e: e.reciprocal(out=rs[:], in_=rs0[:]), reads=[rs0r], writes=[rsr])
        for k in range(KC):
            S.run("dve", lambda e, k=k: e.scalar_tensor_tensor(
                out=out[:, k, :], in0=Xt[:, k, :], scalar=self.g32[:, gcol + k:gcol + k + 1], in1=rs[:],
                op0=ALU.mult, op1=ALU.mult), reads=[Xr[k], rsr, self.g32r], writes=[out_regs[k]])

    def ffn(self, Xt, Xr, gcol, widx, H, Hr, prenormed=False, hook=None):
        S, W = self.S, self.W
        A, Ar = self.A, self.Ar
        if not prenormed:
            self.norm(Xt, Xr, gcol, H, Hr)
        for i in range(11):
            Wt, Wr = W.next(("w_up", widx, i))
            for jj in range(2):
                j = 2 * i + jj
                bg, bgr = self.ps_next()
                self.mm(bg, bgr, [(Wt[:, k * 512 + jj * 128:k * 512 + jj * 128 + 128], H[:, k, :]) for k in range(KC)],
                        reads=[Wr] + Hr)
                bu, bur = self.ps_next()
                self.mm(bu, bur, [(Wt[:, k * 512 + 256 + jj * 128:k * 512 + 256 + jj * 128 + 128], H[:, k, :]) for k in range(KC)],
                        reads=[Wr] + Hr)
                sg, sgr = self.SG()
                S.run("act", lambda e, sg=sg, bg=bg: e.activation(out=sg[:], in_=bg, func=AF.Silu), reads=[bgr], writes=[sgr])
                S.run("dve", lambda e, sg=sg, bu=bu, j=j: e.tensor_tensor(out=A[:, j, :], in0=bu, in1=sg[:], op=ALU.mult),
                      reads=[bur, sgr], writes=[Ar[j]])
        if hook is not None:
            hook()
        for j in range(KC):
            Wd, Wdr = W.next(("w_down", widx, j))
            b, br = self.ps_next()
            self.mm(b, br, [(Wd[:, k * 128:(k + 1) * 128], A[:, k, :]) for k in range(FC)], reads=[Wdr] + Ar)
            S.run("dve", lambda e, b=b, j=j: e.scalar_tensor_tensor(
                out=Xt[:, j, :], in0=b, scalar=0.5, in1=Xt[:, j, :], op0=ALU.mult, op1=ALU.add),
                reads=[br], writes=[Xr[j]])

    def row_pass(self, p):
        S, L, NT = self.S, self.L, self.NT
        first, last = p == 0, p == L
        post, pre = p >= 1, p < L
        full = self.mode == "full"
        lp, ln = p - 1, p

        def gcol(l, which):
            return (3 * l + which) * 8

        gfin = 3 * L * 8
        plan1 = []
        if post:
            if full:
                plan1 += [("w_gate", lp, i) for i in range(4)] + [("w_papf", lp, i) for i in range(2)]
                plan1 += [("w_o", lp, i) for i in range(2)]
            plan1 += [("w_up", 2 * lp + 1, i) for i in range(11)] + [("w_down", 2 * lp + 1, j) for j in range(8)]
        if pre:
            plan1 += [("w_up", 2 * ln, i) for i in range(11)] + [("w_down", 2 * ln, j) for j in range(8)]
            if full:
                plan1 += [("w_inp", ln, i) for i in (0, 2, 4, 5)]
        self.W = self.WRing(self, 5, plan1 * NT)
        W = self.W

        X = [self.psb(f"X{i}", [P, KC, T], F32) for i in range(2)]
        Xr = [[Reg(f"X{i}_{k}") for k in range(KC)] for i in range(2)]
        Hs = [self.psb(f"H{i}", [P, KC, T], BF16) for i in range(2)]
        Hrs = [[Reg(f"H{i}_{k}") for k in range(KC)] for i in range(2)]
        self.A = self.psb("A", [P, FC, T], BF16)
        self.Ar = [Reg(f"A{k}") for k in range(FC)]
        self.SQ = self.ring("SQ", 4, [P, T], BF16)
        self.RS = self.ring("RS", 4, [P, T], F32)
        self.SG = self.ring("SG", 2, [P, T], F32)
        if first:
            XIN = [self.psb(f"XIN{i}", [P, 4, D], F32) for i in range(2)]
            XINr = [Reg(f"XIN{i}") for i in range(2)]
        if post and full:
            G = self.psb("G", [P, 16, T], BF16)
            Gr = [Reg(f"G{k}") for k in range(16)]
            M = self.psb("M", [P, KC, T], BF16)
            Mr = [Reg(f"M{k}") for k in range(KC)]
            OF = [self.psb(f"OF{i}", [P, 8, T], BF16) for i in range(2)]
            OFr = [Reg(f"OF{i}") for i in range(2)]
        if (post and full) or (pre and full):
            TMP = self.ring("TMP", 4, [P, T], F32)
        if pre and full:
            QK = self.psb("QK", [P, 8, T], BF16)
            QKr = [Reg(f"QK{k}") for k in range(8)]
            U = self.psb("U", [P, 4, T], BF16)
            Ur = [Reg(f"U{k}") for k in range(4)]
            V = self.psb("V", [P, 4, T], BF16)
            Vr = [Reg(f"V{k}") for k in range(4)]
            CS = [self.psb(f"CS{i}", [P, 2, T], F32) for i in range(2)]
            QB = self.ring("QB", 2, [P, T], BF16)
            CSr = [Reg(f"CS{i}") for i in range(2)]
        if last:
            YO = self.ring("YO", 2, [P, D], F32)

        def tok(t):
            return slice(t * T, (t + 1) * T)

        def issue_loads(t):
            b = t % 2
            if first:
                S.dma("sp", [(XIN[b][:], self.x_in[tok(t), :].rearrange("(s p) d -> p s d", p=P))], XINr[b], True)
            else:
                S.dma("sp", [(X[b][:], self.xT[:, :, tok(t)].rearrange("k p n -> p k n"))], Xr[b][0], True,
                      writes=Xr[b][1:])
            if post and full:
                S.dma("sp", [(OF[b][:, 0:4, :], self.oT[:, :, tok(t)].rearrange("k p n -> p k n")),
                             (OF[b][:, 4:8, :], self.fT[:, :, tok(t)].rearrange("k p n -> p k n"))], OFr[b], True)
            if pre and full:
                S.dma("sp", [(CS[b][:], self.rope[:, :, tok(t)].rearrange("k p n -> p k n"))], CSr[b], True)

        def prep_first(t):
            b = t % 2
            Xt, Xtr = X[b], Xr[b]
            if first:
                for k in range(KC):
                    bank, br = self.ps_next()
                    for s in range(4):
                        S.run("pe", lambda e, bank=bank, s=s, k=k: e.transpose(
                            bank[:, s * P:(s + 1) * P], XIN[b][:, s, k * P:(k + 1) * P], self.ident[:]),
                            reads=[XINr[b], self.identr], writes=([br] if s in (0, 3) else []))
                    S.run("act", lambda e, bank=bank, k=k: e.activation(out=Xt[:, k, :], in_=bank, func=AF.Copy),
                          reads=[br], writes=[Xtr[k]])
            if post and full:
                g0 = gcol(lp, 1)
            elif post:
                g0 = gcol(lp, 2)
            else:
                g0 = gcol(ln, 0)
            self.norm(Xt, Xtr, g0, Hs[b], Hrs[b])

        def do_tile(t):
            b = t % 2
            Xt, Xtr = X[b], Xr[b]
            H, Hr = Hs[b], Hrs[b]
            if t + 1 < NT:
                issue_loads(t + 1)
            state = {"done": False}

            def early_next():
                if not state["done"] and t + 1 < NT:
                    prep_first(t + 1)
                state["done"] = True

            last_stage = "proj" if (pre and full) else ("ffn1" if pre else "ffn2")
            if post:
                if full:
                    for i in range(4):
                        Wt, Wr = W.next(("w_gate", lp, i))
                        for c in range(4):
                            j = 4 * i + c
                            bk, bkr = self.ps_next()
                            self.mm(bk, bkr, [(Wt[:, k * 512 + c * 128:k * 512 + c * 128 + 128], H[:, k, :]) for k in range(KC)],
                                    reads=[Wr] + Hr)
                            S.run("act", lambda e, bk=bk, j=j: e.activation(out=G[:, j, :], in_=bk, func=AF.Sigmoid),
                                  reads=[bkr], writes=[Gr[j]])
                    Wpa, Wpar = W.next(("w_papf", lp, 0))
                    Wpf, Wpfr = W.next(("w_papf", lp, 1))
                    OFt = OF[b]
                    for j in range(KC):
                        ba, bar = self.ps_next()
                        self.mm(ba, bar, [(Wpa[:, k * 1024 + j * 128:k * 1024 + j * 128 + 128], OFt[:, k, :]) for k in range(4)],
                                reads=[Wpar, OFr[b]])
                        bf, bfr = self.ps_next()
                        self.mm(bf, bfr, [(Wpf[:, k * 1024 + j * 128:k * 1024 + j * 128 + 128], OFt[:, 4 + k, :]) for k in range(4)],
                                reads=[Wpfr, OFr[b]])
                        t1, t1r = TMP()
                        t2, t2r = TMP()
                        S.run("dve", lambda e, ba=ba, t1=t1, j=j: e.tensor_tensor(out=t1[:], in0=ba, in1=G[:, j, :], op=ALU.mult),
                              reads=[bar, Gr[j]], writes=[t1r])
                        S.run("dve", lambda e, bf=bf, t2=t2, j=j: e.tensor_tensor(out=t2[:], in0=bf, in1=G[:, 8 + j, :], op=ALU.mult),
                              reads=[bfr, Gr[8 + j]], writes=[t2r])
                        S.run("dve", lambda e, t1=t1, t2=t2, j=j: e.tensor_tensor(out=M[:, j, :], in0=t1[:], in1=t2[:], op=ALU.add),
                              reads=[t1r, t2r], writes=[Mr[j]])
                    for i in range(2):
                        Wo, Wor = W.next(("w_o", lp, i))
                        for c in range(4):
                            j = 4 * i + c
                            bk, bkr = self.ps_next()
                            self.mm(bk, bkr, [(Wo[:, k * 512 + c * 128:k * 512 + c * 128 + 128], M[:, k, :]) for k in range(KC)],
                                    reads=[Wor] + Mr)
                            S.run("dve", lambda e, bk=bk, j=j: e.tensor_tensor(out=Xt[:, j, :], in0=bk, in1=Xt[:, j, :], op=ALU.add),
                                  reads=[bkr], writes=[Xtr[j]])
                self.ffn(Xt, Xtr, gcol(lp, 2), 2 * lp + 1, H, Hr, prenormed=(not full),
                         hook=(early_next if last_stage == "ffn2" else None))
            if pre:
                self.ffn(Xt, Xtr, gcol(ln, 0), 2 * ln, H, Hr, prenormed=(not post),
                         hook=(early_next if last_stage == "ffn1" else None))
                if full:
                    self.norm(Xt, Xtr, gcol(ln, 1), H, Hr)
                    early_next()
                    CSt, CStr = CS[b], CSr[b]
                    for qk in range(2):
                        Wa, War = W.next(("w_inp", ln, 2 * qk))
                        for c in range(4):
                            b1, b1r = self.ps_next()
                            self.mm(b1, b1r, [(Wa[:, k * 512 + c * 128:k * 512 + c * 128 + 128], H[:, k, :]) for k in range(KC)],
                                    reads=[War] + Hr)
                            qb, qbr = QB()
                            S.run("act", lambda e, b1=b1, qb=qb: e.activation(out=qb[:], in_=b1, func=AF.Copy), reads=[b1r], writes=[qbr])
                            b2, b2r = self.ps_next()
                            self.mm(b2, b2r, [(self.permT[:], qb[:])], reads=[qbr, self.permr])
                            t1, t1r = TMP()
                            t2, t2r = TMP()
                            S.run("dve", lambda e, b1=b1, t1=t1: e.tensor_tensor(out=t1[:], in0=b1, in1=CSt[:, 0, :], op=ALU.mult),
                                  reads=[b1r, CStr, qbr], writes=[t1r])
                            S.run("dve", lambda e, b2=b2, t2=t2: e.tensor_tensor(out=t2[:], in0=b2, in1=CSt[:, 1, :], op=ALU.mult),
                                  reads=[b2r, CStr], writes=[t2r])
                            jj = 4 * qk + c
                            S.run("dve", lambda e, t1=t1, t2=t2, jj=jj: e.tensor_tensor(out=QK[:, jj, :], in0=t1[:], in1=t2[:], op=ALU.add),
                                  reads=[t1r, t2r], writes=[QKr[jj]])
                    Wu, Wur = W.next(("w_inp", ln, 4))
                    for c in range(4):
                        bk, bkr = self.ps_next()
                        self.mm(bk, bkr, [(Wu[:, k * 512 + c * 128:k * 512 + c * 128 + 128], H[:, k, :]) for k in range(KC)],
                                reads=[Wur] + Hr)
                        S.run("act", lambda e, bk=bk, c=c: e.activation(out=U[:, c, :], in_=bk, func=AF.Copy),
                              reads=[bkr], writes=[Ur[c]])
                    Wv, Wvr = W.next(("w_inp", ln, 5))
                    for s in range(4):
                        bk, bkr = self.ps_next()
                        self.mm(bk, bkr, [(H[:, k, s * P:(s + 1) * P], Wv[:, k * 512:(k + 1) * 512]) for k in range(KC)],
                                reads=[Wvr] + Hr)
                        S.run("act", lambda e, bk=bk, s=s: e.activation(out=V[:, s, :], in_=bk, func=AF.Copy),
                              reads=[bkr], writes=[Vr[s]])
                    S.dma("pool", [(self.qkT[:, :, tok(t)].rearrange("k p n -> p k n"), QK[:])], QKr[0], False, reads=QKr[1:])
                    S.dma("pool", [(self.uT[:, :, tok(t)].rearrange("k p n -> p k n"), U[:])], Ur[0], False, reads=Ur[1:])
                    S.dma("pool", [(self.vtok[tok(t), :].rearrange("(s p) e -> p s e", p=P), V[:])], Vr[0], False, reads=Vr[1:])
                S.dma("pool", [(self.xT[:, :, tok(t)].rearrange("k p n -> p k n"), Xt[:])], Xtr[0], False, reads=Xtr[1:])
            if last:
                self.norm(Xt, Xtr, gfin, Xt, Xtr)
                for s in range(4):
                    yo, yor = YO()
                    for half in range(2):
                        bank, br = self.ps_next()
                        for kk in range(4):
                            k = half * 4 + kk
                            S.run("pe", lambda e, bank=bank, s=s, k=k, kk=kk: e.transpose(
                                bank[:, kk * P:(kk + 1) * P], Xt[:, k, s * P:(s + 1) * P], self.ident[:]),
                                reads=[Xtr[k], self.identr], writes=([br] if kk in (0, 3) else []))
                        S.run("act", lambda e, bank=bank, yo=yo, half=half: e.activation(
                            out=yo[:, half * 512:(half + 1) * 512], in_=bank, func=AF.Copy),
                            reads=[br], writes=[yor])
                    S.dma("pool", [(self.y_out[t * T + s * P:t * T + (s + 1) * P, :], yo[:])], yor, False)

        issue_loads(0)
        prep_first(0)
        for t in range(NT):
            do_tile(t)

    def ps_ring(self, banks):
        state = {"i": 0}

        def nxt():
            b = banks[state["i"] % len(banks)]
            state["i"] += 1
            return self.ps[:, b, :], self.psr[b]

        return nxt

    def mix_phase(self, l):
        self.attention(l)
        self.S.barrier()
        self.fourier(l)

    def attention(self, l):
        S, CH, NTC = self.S, self.CH, self.NTC
        if l + 1 < self.L:
            self.issue_conv(l + 1)
        LA = 3
        POOL_EVERY = self.cfg.get("pool_every", 2)
        nbc = CH // P
        sp_state = {"i": 0}

        def SPAIR():
            p = sp_state["i"] % 2
            sp_state["i"] += 1
            return p, self.psr[2 * p]

        acc_pairs = [4, 6]
        KT = [self.psb(f"KT{i}", [P, 2 * CH], BF16) for i in range(2)]
        KTr = [Reg(f"KT{i}") for i in range(2)]
        VH = [self.psb(f"VH{i}", [P, 2 * nbc, P], BF16) for i in range(2)]
        VHr = [Reg(f"VH{i}") for i in range(2)]
        QT = [self.psb(f"QT{i}", [P, T], BF16) for i in range(2)]
        QTr = [Reg(f"QT{i}") for i in range(2)]
        ER = self.ring("E", LA + 3, [P, 2, T], BF16)
        ACCD = [self.psb(f"ACCD{i}", [P, 2, T], F32) for i in range(2)]
        ACCDr = [Reg("accd") for i in range(2)]
        ACCP = [self.psb(f"ACCP{i}", [P, 2, T], F32) for i in range(2)]
        ACCPr = [Reg("accp") for i in range(2)]
        ACCB = self.ring("ACCB", 2, [P, 2, T], BF16)
        RR = self.ring("RR", 2, [P, 2, T], F32)
        TT = self.ring("TT", 2, [P, 2, T], F32)
        OB = self.ring("OB", 2, [P, T], F32)
        SQ = self.ring("ASQ", 2, [P, T], BF16)
        RS = self.ring("ARS", 4, [P, T], F32)
        OS = self.ring("OS", 2, [P, T], BF16)

        groups = [(0, 2 * CH), (2 * CH, CH)]
        heads = []
        for (g0, nk) in groups:
            for h in range(NH):
                heads.append((g0, nk, h))
        items = []
        qunits = []
        for hi, (g0, nk, h) in enumerate(heads):
            for qt in range(nk // T):
                qi = len(qunits)
                qunits.append((hi, g0 + qt * T))
                nkb = nk // P
                for kb in range(nkb):
                    items.append(dict(hi=hi, qi=qi, kb=kb, nkb=nkb, h=h, g0=g0, q0=g0 + qt * T,
                                      first_h=(qt == 0 and kb == 0), first_q=(kb == 0)))

        def load_head(hi):
            g0, nk, h = heads[hi]
            b = hi % 2
            S.dma("sp", [(KT[b][:, 0:nk], self.qkT[4 + h, :, g0:g0 + nk])], KTr[b], True)
            S.dma("sp", [(VH[b][:, 0:nk // P, :], self.vtok[g0:g0 + nk, h * P:(h + 1) * P].rearrange("(kb p) e -> p kb e", p=P))],
                  VHr[b], True)

        def load_q(qi):
            hi, q0 = qunits[qi]
            h = heads[hi][2]
            S.dma("sp", [(QT[qi % 2][:], self.qkT[h, :, q0:q0 + T])], QTr[qi % 2], True)

        deferred = []

        def stage1(it):
            hb, qb = it["hi"] % 2, it["qi"] % 2
            kb = it["kb"]
            p, pr = SPAIR()
            E, Er = ER()
            it["E"], it["Er"] = E, Er
            S.run("pe", [lambda e, c=c: e.matmul(self.ps[:, 2 * p + c, :], KT[hb][c * 64:(c + 1) * 64, kb * P:(kb + 1) * P],
                                                 QT[qb][c * 64:(c + 1) * 64, :], start=True, stop=True) for c in range(2)],
                  reads=[KTr[hb], QTr[qb]], writes=[pr])
            cross = (it["g0"] == 0) and ((kb // nbc) != ((it["q0"] // T) // NTC))
            bias = self.xb if cross else self.zero1
            S.run("act", lambda e: e.activation(out=E[:], in_=self.ps[:, 2 * p:2 * p + 2, :], func=AF.Exp, bias=bias[:], scale=0.125),
                  reads=[pr, self.xbr, self.zero1r], writes=[Er])

        def stage2(it):
            hb = it["hi"] % 2
            kb, nkb = it["kb"], it["nkb"]
            par = it["qi"] % 2
            ob, sbk = 4, 6
            Or, Sr = self.psr[ob], self.psr[sbk]
            E, Er = it["E"], it["Er"]
            S.run("pe", [lambda e, c=c: e.matmul(self.ps[:, ob + c, :], VH[hb][:, kb, :], E[:, c, :], start=(kb == 0), stop=(kb == nkb - 1))
                         for c in range(2)],
                  reads=[VHr[hb], Er], writes=([Or] if kb in (0, nkb - 1) else []))
            m = kb % 4
            if m in (1, 3):
                S.run("pe", [lambda e, c=c: e.matmul(self.ps[:, sbk + c, :], self.ones[:], E[:, c, :], start=(kb == 1), stop=False)
                             for c in range(2)],
                      reads=[Er, self.onesr], writes=([Sr] if kb == 1 else []))
            else:
                use_pool = (m == 2)
                eng = "pool" if use_pool else "dve"
                acc, accr = (ACCP[par], ACCPr[par]) if use_pool else (ACCD[par], ACCDr[par])
                if kb < 4:
                    S.run(eng, lambda e: e.tensor_copy(out=acc[:], in_=E[:]), reads=[Er], writes=[accr])
                else:
                    S.run(eng, lambda e: e.tensor_tensor(out=acc[:], in0=acc[:], in1=E[:], op=ALU.add), reads=[Er], writes=[accr])
            if kb == nkb - 1:
                h, q0 = it["h"], it["q0"]
                ab, abr = ACCB()
                S.run("dve", lambda e: e.tensor_tensor(out=ab[:], in0=ACCD[par][:], in1=ACCP[par][:], op=ALU.add),
                      reads=[ACCDr[par], ACCPr[par]], writes=[abr])
                S.run("pe", [lambda e, c=c: e.matmul(self.ps[:, sbk + c, :], self.ones[:], ab[:, c, :], start=False, stop=True)
                             for c in range(2)], reads=[abr, self.onesr], writes=[Sr])
                R, Rr = RR()
                S.run("dve", lambda e: e.reciprocal(out=R[:], in_=self.ps[:, sbk:sbk + 2, :]), reads=[Sr], writes=[Rr])
                tt, ttr = TT()
                S.run("dve", lambda e: e.tensor_tensor(out=tt[:], in0=self.ps[:, ob:ob + 2, :], in1=R[:], op=ALU.mult),
                      reads=[Or, Rr], writes=[ttr])

                def fin():
                    o, orr = OB()
                    S.run("dve", lambda e: e.scalar_tensor_tensor(out=o[:], in0=tt[:, 1, :], scalar=self.neglam[:, l:l + 1], in1=tt[:, 0, :],
                                                                  op0=ALU.mult, op1=ALU.add),
                          reads=[ttr, self.scalr], writes=[orr])
                    sq, sqr = SQ()
                    S.run("act", lambda e: e.activation(out=sq[:], in_=o[:], func=AF.Square), reads=[orr], writes=[sqr])
                    p2, p2r = SPAIR()
                    bk = self.ps[:, 2 * p2, :]
                    S.run("pe", lambda e: e.matmul(bk, self.ones[:], sq[:], start=True, stop=True), reads=[sqr, self.onesr], writes=[p2r])
                    r0, r0r = RS()
                    r1, r1r = RS()
                    S.run("act", lambda e: e.activation(out=r0[:], in_=bk, func=AF.Sqrt, bias=self.epst[:], scale=1.0 / P),
                          reads=[p2r, self.epsr], writes=[r0r])
                    S.run("dve", lambda e: e.reciprocal(out=r1[:], in_=r0[:]), reads=[r0r], writes=[r1r])
                    os_, osr = OS()
                    S.run("dve", lambda e: e.scalar_tensor_tensor(out=os_[:], in0=o[:], scalar=self.gs[:, l:l + 1], in1=r1[:],
                                                                  op0=ALU.mult, op1=ALU.mult),
                          reads=[orr, r1r, self.scalr], writes=[osr])
                    S.dma("pool", [(self.oT[h, :, q0:q0 + T], os_[:])], osr, False)

                deferred.append([5, fin])

        n = len(items)
        load_head(0)
        load_q(0)
        for idx in range(n + LA):
            if idx < n:
                stage1(items[idx])
            if idx >= LA:
                it = items[idx - LA]
                stage2(it)
                if it["first_h"] and it["hi"] + 1 < len(heads):
                    load_head(it["hi"] + 1)
                if it["first_q"] and it["qi"] + 1 < len(qunits):
                    load_q(it["qi"] + 1)
            for d in deferred:
                d[0] -= 1
            ready = [d for d in deferred if d[0] <= 0]
            for d in ready:
                deferred.remove(d)
                d[1]()
        while deferred:
            deferred.pop(0)[1]()

    def fourier(self, l):
        S, CH = self.S, self.CH
        nb = CH // P
        half = nb // 2
        hc = CH // 2
        PB = self.ps_ring([0, 1, 2, 3, 4, 5, 6, 7])
        UT = [self.psb(f"UT{i}", [P, self.NTOK], BF16) for i in range(2)]
        UTr = [Reg(f"UT{i}") for i in range(2)]
        PQ = [self.psb(f"PQ{g}", [P, nb, 256], BF16) for g in range(4)]
        PQr = [[Reg(f"PQ{g}_{i}") for i in range(nb // 2)] for g in range(4)]
        MT = [[self.psb(f"MT{i}_{cs}", [P, nb, T], BF16) for cs in range(2)] for i in range(2)]
        MTr = [Reg(f"MT{i}") for i in range(2)]
        FS = self.ring("FS", 3, [P, T], BF16)
        ui = 0
        mi = 0
        for X in range(3):
            for g in range(4):
                ub = ui % 2
                ui += 1
                S.dma("sp", [(UT[ub][:], self.uT[g, :, :])], UTr[ub], True)
                for rp in range(nb // 2):
                    bank, br = PB()
                    for sub in range(2):
                        rb = 2 * rp + sub
                        if X == 2:
                            terms = [(2 * CH + rb * P, 0)]
                        else:
                            j0 = (rb % half) * P
                            if X == 0:
                                terms = [(j0, 0), (hc + j0, 1)] if rb < half else [(hc + j0, 2), (CH + j0, 1), (CH + hc + j0, 1)]
                            else:
                                terms = [(j0, 1), (hc + j0, 3), (CH + j0, 2)] if rb < half else [(CH + j0, 1), (CH + hc + j0, 4)]
                        nt = len(terms)
                        for ti, (tk0, ci) in enumerate(terms):
                            tracked = (sub == 0 and ti == 0) or (sub == 1 and ti == nt - 1)
                            S.run("pe", lambda e, bank=bank, sub=sub, tk0=tk0, ci=ci, ti=ti, nt=nt, ub=ub: e.matmul(
                                bank[:, sub * 256:(sub + 1) * 256], UT[ub][:, tk0:tk0 + P], self.cs5t[:, ci, :],
                                start=(ti == 0), stop=(ti == nt - 1)),
                                reads=[UTr[ub], self.cs5r], writes=([br] if tracked else []))
                    S.run("act", lambda e, bank=bank, g=g, rp=rp: e.activation(
                        out=PQ[g][:, 2 * rp:2 * rp + 2, :], in_=bank.rearrange("p (a b) -> p a b", a=2), func=AF.Copy),
                        reads=[br], writes=[PQr[g][rp]])
            for ct in range(CH // T):
                mb = mi % 2
                mi += 1
                S.dma("sp", [(MT[mb][cs][:], self.mtab[X, cs, :, ct * T:(ct + 1) * T].rearrange("(b p) n -> p b n", p=P))
                             for cs in range(2)], MTr[mb], True)
                for g in range(4):
                    bank, br = PB()
                    pairs = []
                    for blk in range(nb):
                        pairs.append((PQ[g][:, blk, 0:P], MT[mb][0][:, blk, :]))
                        pairs.append((PQ[g][:, blk, P:2 * P], MT[mb][1][:, blk, :]))
                    self.mm(bank, br, pairs, reads=[MTr[mb]] + PQr[g])
                    fs, fsr = FS()
                    S.run("dve", lambda e, bank=bank, fs=fs: e.tensor_copy(out=fs[:], in_=bank), reads=[br], writes=[fsr])
                    S.dma("pool", [(self.fT[g, :, X * CH + ct * T:X * CH + (ct + 1) * T], fs[:])], fsr, False)


def _assignment(n_link, n_unl):
    cores = []
    for i in range(n_link):
        cores.append(("L", i, i))
    for j in range(n_unl):
        cores.append(("U", [n_link + 3 * j + c for c in range(3)]))
    return cores


def _slot_positions(kind, CH):
    j = np.arange(CH)
    if kind == "L":
        return np.concatenate([2 * j, 2 * j + 1, j])
    return np.concatenate([j, j, j])


def _rope_table(pos):
    inv = (1.0 / (10000.0 ** (np.arange(0, 64, 2, dtype=np.float32) / np.float32(64)))).astype(np.float32)
    ang = pos.astype(np.float32)[:, None] * inv[None, :]
    ang = np.concatenate([ang, ang], -1)
    c = np.cos(ang).astype(np.float32)
    s = np.sin(ang).astype(np.float32)
    sign = np.where(np.arange(64) < 32, -1.0, 1.0).astype(np.float32)
    s = s * sign[None, :]
    c2 = np.concatenate([c, c], -1).T
    s2 = np.concatenate([s, s], -1).T
    return np.ascontiguousarray(np.stack([c2, s2], 0))


def _dft_tables(kind, CH):
    r = np.arange(CH, dtype=np.int64)
    t = np.arange(CH, dtype=np.int64)

    def tab(mk, S, norm):
        ang = 2.0 * np.pi * (mk % S).astype(np.float64) / S
        return np.stack([np.cos(ang) * norm, -np.sin(ang) * norm], 0)

    nC = 1.0 / math.sqrt(CH * 128.0)
    MC = tab(r[:, None] * t[None, :], CH, nC)
    if kind == "U":
        out = np.stack([MC, MC, MC], 0)
    else:
        S2 = 2 * CH
        h = CH // 2
        m = np.where(r < h, 2 * r, 2 * (r - h) + 1)
        nL = 1.0 / math.sqrt(S2 * 128.0)
        MA = tab(m[:, None] * (2 * t[None, :]), S2, nL)
        MB = tab(m[:, None] * (2 * t[None, :] + 1), S2, nL)
        out = np.stack([MA, MB, MC], 0)
    return out.astype(np.float32).astype(ml_dtypes.bfloat16)


def _cs5(kind):
    f = 1.0 if kind == "L" else 0.0
    c = np.arange(128, dtype=np.int64)
    ang = 2.0 * np.pi * ((c[:, None] * c[None, :]) % 128).astype(np.float64) / 128.0
    CSm = np.concatenate([np.cos(ang), np.sin(ang)], 1)
    coefs = [1.0, f, 1.0 - f, -f, 1.0 - 2.0 * f]
    return np.stack([CSm * cf for cf in coefs], 0).astype(np.float32)


def make_in_maps(cfg, inp, cores):
    L, CH = cfg["depth"], cfg["ch"]
    xp, xs = np.asarray(inp["x_prompt"]), np.asarray(inp["x_sample"])
    shared = _layout_weights(inp, L)
    gcols = []
    for l in range(L):
        for nm in ("g_ff1", "g_mix", "g_ff2"):
            gcols.append(np.asarray(inp[nm])[l].reshape(KC, P).T)
    gcols.append(np.asarray(inp["g_final"]).reshape(KC, P).T)
    shared["gvec"] = np.ascontiguousarray(np.concatenate(gcols, 1))
    shared["gsub"] = np.ascontiguousarray(np.asarray(inp["g_sub"])[:L].T)
    shared["lamv"] = np.ascontiguousarray(np.stack([np.asarray(inp[k])[:L] for k in ("lam_q1", "lam_k1", "lam_q2", "lam_k2")], 0))
    shared["ident_in"] = np.eye(P, dtype=np.float32)
    pm = np.zeros((P, P), np.float32)
    mm_ = np.arange(P)
    pm[(mm_ // 64) * 64 + (mm_ % 64 + 32) % 64, mm_] = 1.0
    shared["perm_in"] = pm
    per_kind = {}
    for kind in ("L", "U"):
        per_kind[kind] = dict(
            rope=_rope_table(_slot_positions(kind, CH)),
            xbias=np.full((P, 1), 0.0 if kind == "L" else NEG, np.float32),
            cs5=_cs5(kind),
            mtab=_dft_tables(kind, CH),
        )
    maps = []
    for c in cores:
        m = dict(shared)
        m.update(per_kind[c[0]])
        if c[0] == "L":
            x = np.concatenate([xs[c[1]][0::2], xs[c[1]][1::2], xp[c[2]]], 0)
        else:
            x = np.concatenate([xp[i] for i in c[1]], 0)
        m["x_in"] = np.ascontiguousarray(x)
        maps.append(m)
    return maps


def gather_outputs(cfg, results, cores, n_prompt, n_sample):
    CH = cfg["ch"]
    yp = np.zeros((n_prompt, CH, D), np.float32)
    ys = np.zeros((n_sample, 2 * CH, D), np.float32)
    for c, r in zip(cores, results):
        y = r["y_out"]
        if c[0] == "L":
            ys[c[1]][0::2] = y[0:CH]
            ys[c[1]][1::2] = y[CH:2 * CH]
            yp[c[2]] = y[2 * CH:3 * CH]
        else:
            for i, pi in enumerate(c[1]):
                yp[pi] = y[i * CH:(i + 1) * CH]
    return yp, ys


def kernel(**inputs):
    cfg = dict(FULL_CFG)
    cores = _assignment(4, 4)
    nc = Builder(cfg).build()
    maps = make_in_maps(cfg, inputs, cores)
    res = run_bass_kernel_spmd(nc, maps, core_ids=list(range(8)))
    yp, ys = gather_outputs(cfg, res.results, cores, 16, 4)
    return (yp, ys)
```
